# Optimizing a Trainium2 kernel written in Bass

```python
import math
import jax, jax.numpy as jnp
from jax import lax
import numpy as np

D_MODEL = 1024
BATCH = 8
SEQ = 2048
DEPTH = 4
DEC_BATCH = 128
DEC_SEQ = 1
PAST_LEN = 16384
PAGE_SIZE = 128

N_MIXERS = 2
N_SSM_LAYERS = (DEPTH + 1) // 2
N_CONV_LAYERS = DEPTH // 2
GROUP_SIZE = 16
N_GROUPS = D_MODEL // GROUP_SIZE
STATE_DIM = 64
CONV_WIDTH = 3
D_FF = 4 * D_MODEL
RMS_EPS = 1e-6
DT_MIN = 1e-3
DT_MAX = 1e-1

kernel_name = "s5_shortconv_hybrid_decode_step"


def rms_norm(x, g):
    xf = x.astype(jnp.float32)
    ms = jnp.mean(xf * xf, axis=-1, keepdims=True)
    return (xf * lax.rsqrt(ms + RMS_EPS) * g.astype(jnp.float32)).astype(x.dtype)


def _ssm_combine(left, right):
    a_l, b_l = left
    a_r, b_r = right
    return a_r * a_l, a_r * b_l + b_r


def s5_mixer(u, h0_re, h0_im, a_re, a_im, log_dt, b_re, b_im, c_re, c_im, d_skip, w_glu):
    bt, s, _ = u.shape
    uf = u.astype(jnp.float32).reshape(bt, s, N_GROUPS, GROUP_SIZE)
    a = lax.complex(a_re.astype(jnp.float32), a_im.astype(jnp.float32))
    dt = jnp.exp(log_dt.astype(jnp.float32))[:, None]
    a_bar = jnp.exp(a * dt)
    b_c = lax.complex(b_re.astype(jnp.float32), b_im.astype(jnp.float32))
    c_c = lax.complex(c_re.astype(jnp.float32), c_im.astype(jnp.float32))
    b_bar = ((a_bar - 1.0) / a)[..., None] * b_c
    bu = jnp.einsum('gph,bsgh->bsgp', b_bar, uf.astype(jnp.complex64))
    h0 = lax.complex(h0_re.astype(jnp.float32), h0_im.astype(jnp.float32))
    bu = bu.at[:, 0].add(a_bar[None] * h0)
    a_seq = jnp.broadcast_to(a_bar, bu.shape)
    _, h = lax.associative_scan(_ssm_combine, (a_seq, bu), axis=1)
    y = jnp.einsum('ghp,bsgp->bsgh', c_c, h).real + d_skip.astype(jnp.float32).reshape(N_GROUPS, GROUP_SIZE) * uf
    y = jax.nn.gelu(y.reshape(bt, s, D_MODEL))
    z = y @ w_glu.astype(jnp.float32)
    out = z[..., :D_MODEL] * jax.nn.sigmoid(z[..., D_MODEL:])
    h_last = h[:, -1]
    return out.astype(u.dtype), h_last.real, h_last.imag


def short_conv_mixer(u, buf, w_in, conv_w, w_out):
    s = u.shape[1]
    bcv = u @ w_in
    gate_b = bcv[..., :D_MODEL]
    gate_c = bcv[..., D_MODEL:2 * D_MODEL]
    v = bcv[..., 2 * D_MODEL:]
    cv = gate_c * v
    xp = jnp.concatenate([buf.astype(cv.dtype), cv], axis=1)
    y = conv_w[0] * xp[:, 0:s]
    for k in range(1, CONV_WIDTH):
        y = y + conv_w[k] * xp[:, k:k + s]
    out = (gate_b * y) @ w_out
    return out, xp[:, -(CONV_WIDTH - 1):]


def squared_relu_mlp(x, w_up, w_down):
    h = jax.nn.relu(x @ w_up)
    return (h * h) @ w_down


def trunk(x, h_re, h_im, conv_buf, norm_mix, norm_mlp, norm_final,
          ssm_a_re, ssm_a_im, ssm_log_dt, ssm_b_re, ssm_b_im, ssm_c_re, ssm_c_im, ssm_d, ssm_w_glu,
          conv_w_in, conv_w, conv_w_out, mlp_w_up, mlp_w_down):
    new_re, new_im, new_buf = [], [], []
    for i in range(DEPTH):
        j = i // N_MIXERS
        hn = rms_norm(x, norm_mix[i])
        if i % N_MIXERS == 0:
            m, hr, hi = s5_mixer(hn, h_re[j], h_im[j], ssm_a_re[j], ssm_a_im[j], ssm_log_dt[j],
                                 ssm_b_re[j], ssm_b_im[j], ssm_c_re[j], ssm_c_im[j], ssm_d[j], ssm_w_glu[j])
            new_re.append(hr)
            new_im.append(hi)
        else:
            m, nb = short_conv_mixer(hn, conv_buf[j], conv_w_in[j], conv_w[j], conv_w_out[j])
            new_buf.append(nb)
        x = x + m
        x = x + squared_relu_mlp(rms_norm(x, norm_mlp[i]), mlp_w_up[i], mlp_w_down[i])
    y = rms_norm(x, norm_final)
    return y, jnp.stack(new_re), jnp.stack(new_im), jnp.stack(new_buf)


def setup_inputs(seed: int = 0) -> dict:
    key = jax.random.key(seed)
    ks = jax.random.split(key, 24)
    f32 = jnp.float32
    n = jnp.arange(STATE_DIM, dtype=f32)
    a_re = -0.5 + 0.01 * jax.random.normal(ks[5], (N_SSM_LAYERS, N_GROUPS, STATE_DIM), f32)
    a_im = math.pi * n + 0.01 * jax.random.normal(ks[6], (N_SSM_LAYERS, N_GROUPS, STATE_DIM), f32)
    log_dt = jax.random.uniform(ks[7], (N_SSM_LAYERS, N_GROUPS), f32, math.log(DT_MIN), math.log(DT_MAX))
    b_scale = (2.0 * GROUP_SIZE) ** -0.5
    c_scale = (2.0 * STATE_DIM) ** -0.5
    return {
        'x_prompt': jax.random.normal(ks[0], (BATCH, SEQ, D_MODEL), f32),
        'x_sample': jax.random.normal(ks[1], (DEC_BATCH, DEC_SEQ, D_MODEL), f32),
        'state_ssm_re': 0.3 * jax.random.normal(ks[2], (N_SSM_LAYERS, DEC_BATCH, N_GROUPS, STATE_DIM), f32),
        'state_ssm_im': 0.3 * jax.random.normal(ks[3], (N_SSM_LAYERS, DEC_BATCH, N_GROUPS, STATE_DIM), f32),
        'state_conv': jax.random.normal(ks[4], (N_CONV_LAYERS, DEC_BATCH, CONV_WIDTH - 1, D_MODEL), f32),
        'norm_mix': 1.0 + 0.02 * jax.random.normal(ks[8], (DEPTH, D_MODEL), f32),
        'norm_mlp': 1.0 + 0.02 * jax.random.normal(ks[9], (DEPTH, D_MODEL), f32),
        'norm_final': 1.0 + 0.02 * jax.random.normal(ks[10], (D_MODEL,), f32),
        'ssm_a_re': a_re,
        'ssm_a_im': a_im,
        'ssm_log_dt': log_dt,
        'ssm_b_re': b_scale * jax.random.normal(ks[11], (N_SSM_LAYERS, N_GROUPS, STATE_DIM, GROUP_SIZE), f32),
        'ssm_b_im': b_scale * jax.random.normal(ks[12], (N_SSM_LAYERS, N_GROUPS, STATE_DIM, GROUP_SIZE), f32),
        'ssm_c_re': c_scale * jax.random.normal(ks[13], (N_SSM_LAYERS, N_GROUPS, GROUP_SIZE, STATE_DIM), f32),
        'ssm_c_im': c_scale * jax.random.normal(ks[14], (N_SSM_LAYERS, N_GROUPS, GROUP_SIZE, STATE_DIM), f32),
        'ssm_d': 1.0 + 0.1 * jax.random.normal(ks[15], (N_SSM_LAYERS, D_MODEL), f32),
        'ssm_w_glu': D_MODEL ** -0.5 * jax.random.normal(ks[16], (N_SSM_LAYERS, D_MODEL, 2 * D_MODEL), f32),
        'conv_w_in': D_MODEL ** -0.5 * jax.random.normal(ks[17], (N_CONV_LAYERS, D_MODEL, 3 * D_MODEL), f32),
        'conv_w': CONV_WIDTH ** -0.5 * jax.random.normal(ks[18], (N_CONV_LAYERS, CONV_WIDTH, D_MODEL), f32),
        'conv_w_out': D_MODEL ** -0.5 * jax.random.normal(ks[19], (N_CONV_LAYERS, D_MODEL, D_MODEL), f32),
        'mlp_w_up': D_MODEL ** -0.5 * jax.random.normal(ks[20], (DEPTH, D_MODEL, D_FF), f32),
        'mlp_w_down': D_FF ** -0.5 * jax.random.normal(ks[21], (DEPTH, D_FF, D_MODEL), f32),
    }


def reference(x_prompt, x_sample, state_ssm_re, state_ssm_im, state_conv, norm_mix, norm_mlp, norm_final,
              ssm_a_re, ssm_a_im, ssm_log_dt, ssm_b_re, ssm_b_im, ssm_c_re, ssm_c_im, ssm_d, ssm_w_glu,
              conv_w_in, conv_w, conv_w_out, mlp_w_up, mlp_w_down):
    zero_re = jnp.zeros((N_SSM_LAYERS, BATCH, N_GROUPS, STATE_DIM), jnp.float32)
    zero_im = jnp.zeros((N_SSM_LAYERS, BATCH, N_GROUPS, STATE_DIM), jnp.float32)
    zero_buf = jnp.zeros((N_CONV_LAYERS, BATCH, CONV_WIDTH - 1, D_MODEL), x_prompt.dtype)
    y_prompt, new_ssm_re_prompt, new_ssm_im_prompt, new_conv_prompt = trunk(
        x_prompt, zero_re, zero_im, zero_buf, norm_mix, norm_mlp, norm_final,
        ssm_a_re, ssm_a_im, ssm_log_dt, ssm_b_re, ssm_b_im, ssm_c_re, ssm_c_im, ssm_d, ssm_w_glu,
        conv_w_in, conv_w, conv_w_out, mlp_w_up, mlp_w_down)
    y_sample, new_ssm_re_sample, new_ssm_im_sample, new_conv_sample = trunk(
        x_sample, state_ssm_re, state_ssm_im, state_conv, norm_mix, norm_mlp, norm_final,
        ssm_a_re, ssm_a_im, ssm_log_dt, ssm_b_re, ssm_b_im, ssm_c_re, ssm_c_im, ssm_d, ssm_w_glu,
        conv_w_in, conv_w, conv_w_out, mlp_w_up, mlp_w_down)
    return (y_prompt, y_sample, new_ssm_re_prompt, new_ssm_im_prompt, new_conv_prompt,
            new_ssm_re_sample, new_ssm_im_sample, new_conv_sample)
```

```python
import math
from contextlib import ExitStack

import numpy as np
import concourse.bass as bass
import concourse.mybir as mybir
from concourse.bass_utils import run_bass_kernel_spmd

F32 = mybir.dt.float32
BF16 = mybir.dt.bfloat16
AF = mybir.ActivationFunctionType
ALU = mybir.AluOpType

ENGS = ("pe", "act", "dve", "pool", "sp")
NT = 2064
NS = 16
BLK = [(0, 416), (416, 832), (832, 1248), (1248, 1664), (1664, 2064)]
EPS = 1e-6
STRICT_SYNC = True


class Reg:
    __slots__ = ("name", "lw", "rs", "excl")

    def __init__(self, name="", excl=False):
        self.name = name
        self.lw = None
        self.rs = []
        self.excl = excl


class DSem:
    __slots__ = ("sem", "cnt")

    def __init__(self, sem):
        self.sem = sem
        self.cnt = 0


class Op:
    __slots__ = ("eng", "fn", "deps", "pos", "needs_inc", "inc_val", "dsem", "dcnt", "is_dma", "relax")

    def __init__(self, eng, fn):
        self.relax = False
        self.eng = eng
        self.fn = fn
        self.deps = []
        self.pos = 0
        self.needs_inc = False
        self.inc_val = 0
        self.dsem = None
        self.dcnt = 0
        self.is_dma = False


class Prog:
    def __init__(self, nc):
        self.nc = nc
        self.ops = {e: [] for e in ENGS}
        self.final_waits = []
        self.pending_barrier = {e: [] for e in ENGS}

    def _track(self, op, reads, writes):
        deps = op.deps
        ex = [r for r in reads if r.excl]
        if ex:
            reads = [r for r in reads if not r.excl]
            writes = list(writes) + [r for r in ex if r not in writes]
        for r in reads:
            if r.lw is not None:
                deps.append(r.lw)
        for w in writes:
            if w.lw is not None:
                deps.append(w.lw)
            deps.extend(w.rs)
        for r in reads:
            r.rs.append(op)
        for w in writes:
            w.lw = op
            w.rs = []

    def barrier(self):
        lasts = [self.ops[e][-1] for e in ENGS if self.ops[e]]
        for e in ("pe", "act", "dve", "pool", "sp"):
            self.pending_barrier[e] = list(lasts)

    def op(self, eng, fn, reads=(), writes=(), relax=False):
        o = Op(eng, fn)
        o.relax = relax
        if self.pending_barrier[eng]:
            o.deps.extend(self.pending_barrier[eng])
            self.pending_barrier[eng] = []
        self._track(o, reads, writes)
        o.pos = len(self.ops[eng])
        self.ops[eng].append(o)
        return o

    def dma(self, eng, dsem, fn, reads=(), writes=(), n=1):
        o = Op(eng, fn)
        if eng != "pool" and self.pending_barrier[eng]:
            o.deps.extend(self.pending_barrier[eng])
            self.pending_barrier[eng] = []
        o.is_dma = True
        o.dsem = dsem
        dsem.cnt += 16 * n
        o.dcnt = dsem.cnt
        self._track(o, reads, writes)
        o.pos = len(self.ops[eng])
        self.ops[eng].append(o)
        return o

    def _need_wait(self, o, d):
        if d.is_dma or d.eng != o.eng:
            return True
        if o.eng == "pe" and not o.is_dma:
            return False
        if o.is_dma or STRICT_SYNC:
            return True
        n = 0
        need = 1 if o.relax else 2
        for x in self.ops[o.eng][d.pos + 1:o.pos]:
            if not x.is_dma:
                n += 1
                if n >= need:
                    return False
        return True

    def emit(self, sems):
        for e in ENGS:
            for o in self.ops[e]:
                o.deps = [d for d in dict.fromkeys(o.deps) if d is not o and self._need_wait(o, d)]
                for d in o.deps:
                    if not d.is_dma:
                        d.needs_inc = True
        for e in ENGS:
            c = 0
            for o in self.ops[e]:
                if o.needs_inc and not o.is_dma:
                    c += 1
                    o.inc_val = c
        nc = self.nc
        with nc.Block() as block:
            def run(ename, eng):
                waited = {}
                for o in self.ops[ename]:
                    for d in o.deps:
                        if d.is_dma:
                            key = ("d", id(d.dsem))
                            sem, val = d.dsem.sem, d.dcnt
                        else:
                            key = ("c", d.eng)
                            sem, val = sems[d.eng], d.inc_val
                        if waited.get(key, 0) >= val:
                            continue
                        waited[key] = val
                        eng.wait_ge(sem, val)
                    r = o.fn(eng)
                    if o.is_dma:
                        for ins in r:
                            ins.then_inc(o.dsem.sem, 16)
                    elif o.needs_inc:
                        r.then_inc(sems[ename], 1)
                if ename == "sp":
                    for ds in self.final_waits:
                        if ds.cnt:
                            eng.wait_ge(ds.sem, ds.cnt)

            @block.tensor
            def _(eng):
                run("pe", eng)

            @block.scalar
            def _(eng):
                run("act", eng)

            @block.vector
            def _(eng):
                run("dve", eng)

            @block.gpsimd
            def _(eng):
                run("pool", eng)

            @block.sync
            def _(eng):
                run("sp", eng)


class T:
    __slots__ = ("ap", "regs")

    def __init__(self, ap, regs):
        self.ap = ap
        self.regs = list(regs) if isinstance(regs, (list, tuple)) else [regs]

    def __getitem__(self, key):
        return T(self.ap[key], self.regs)

    def v(self, fn):
        return T(fn(self.ap), self.regs)


def _rs(v, shape):
    shape = list(shape)
    if len(shape) == 1:
        return v
    if len(shape) == 2:
        return v.rearrange("p (a b) -> p a b", a=shape[0])
    if len(shape) == 3:
        return v.rearrange("p (a b c) -> p a b c", a=shape[0], b=shape[1])
    if len(shape) == 4:
        return v.rearrange("p (a b c d) -> p a b c d", a=shape[0], b=shape[1], c=shape[2])
    raise ValueError(shape)


def _prod(s):
    n = 1
    for x in s:
        n *= x
    return n


AW = 52176
O_XT = 0
O_XN = 16512
O_W = 24768
NSLOT = 5
O_RB = O_W + NSLOT * 2048
O_CONST = O_RB + 16000
RB_SQ = 8256
RB_RS = 12352
RB_LN = 13376


class Builder:
    def __init__(self, layers=(0, 1, 2, 3), s5=True, dbg=False, cut=None):
        self.cut = cut
        self.layers = layers
        self.s5 = s5
        self.dbg = dbg
        self.nc = bass.Bass("TRN2", target_bir_lowering=False)
        self.es = ExitStack()
        self.P = Prog(self.nc)
        self._bank = 0
        self._evac = 0
        self._reserved = set()
        self._norm_done = None
        self._next_norm = None

    def declare(self):
        nc = self.nc

        def din(name, shape):
            return nc.dram_tensor(name, list(shape), F32, kind="ExternalInput").ap()

        def dout(name, shape):
            return nc.dram_tensor(name, list(shape), F32, kind="ExternalOutput").ap()

        self.xp = din("xp", [2048, 1024])
        self.xs = din("xs", [16, 1024])
        self.sre = din("sre", [2, 16, 4096])
        self.sim = din("sim", [2, 16, 4096])
        self.scv = din("scv", [2, 16, 2, 1024])
        self.norm_mix = din("norm_mix", [4, 1024])
        self.norm_mlp = din("norm_mlp", [4, 1024])
        self.norm_final = din("norm_final", [1, 1024])
        self.a_re = din("a_re", [2, 64, 64])
        self.a_im = din("a_im", [2, 64, 64])
        self.log_dt = din("log_dt", [2, 64])
        self.b_re = din("b_re", [2, 64, 64, 16])
        self.b_im = din("b_im", [2, 64, 64, 16])
        self.c_re = din("c_re", [2, 64, 16, 64])
        self.c_im = din("c_im", [2, 64, 16, 64])
        self.ssm_d = din("ssm_d", [2, 1024])
        self.w_glu = din("w_glu", [2, 1024, 2048])
        self.cw_in = din("cw_in", [2, 1024, 3072])
        self.cw = din("cw", [6, 1024])
        self.cw_out = din("cw_out", [2, 1024, 1024])
        self.w_up = din("w_up", [4, 1024, 4096])
        self.w_down = din("w_down", [4, 4096, 1024])

        self.y_p = dout("y_p", [2048, 1024])
        self.y_s = dout("y_s", [16, 1024])
        self.nre_p = dout("nre_p", [2, 4096])
        self.nim_p = dout("nim_p", [2, 4096])
        self.ncv_p = dout("ncv_p", [2, 2, 1024])
        self.nre_s = dout("nre_s", [2, 16, 4096])
        self.nim_s = dout("nim_s", [2, 16, 4096])
        self.ncv_s = dout("ncv_s", [2, 16, 2, 1024])
        if self.dbg:
            self.dbg_xt = dout("dbg_xt", [128, 8 * NT])

        self.scrE = nc.dram_tensor("scrE", [2, 128, 8192], BF16, kind="Internal").ap()
        self.scrT = nc.dram_tensor("scrT", [2, 128, 8192], BF16, kind="Internal").ap()
        self.scrF = nc.dram_tensor("scrF", [2, 128, 8192], BF16, kind="Internal").ap()
        self.scrP = nc.dram_tensor("scrP", [128, 2048], BF16, kind="Internal").ap()

        es = self.es
        self.arena = es.enter_context(nc.sbuf_tensor("arena", [128, AW], F32))
        self.ps = []
        for b in range(8):
            t = es.enter_context(nc.psum_tensor("ps%d" % b, [128, 512], F32))
            self.ps.append(T(t[:, :], Reg("ps%d" % b, excl=True)))
        self.sems = {e: es.enter_context(nc.semaphore("s_" + e)) for e in ENGS}
        self._nds = 0

    def dsem(self, name):
        self._nds += 1
        return DSem(self.es.enter_context(self.nc.semaphore("d_%s_%d" % (name, self._nds))))

    def f32(self, off, shape, regs=None, name=""):
        n = _prod(shape)
        v = self.arena[:, off:off + n]
        return T(_rs(v, shape), regs if regs is not None else Reg(name))

    def bf(self, off, shape, regs=None, name=""):
        n = _prod(shape)
        assert n % 2 == 0
        v = self.arena[:, off:off + n // 2].bitcast(BF16)
        return T(_rs(v, shape), regs if regs is not None else Reg(name))

    def rcols(self, R, k, c0, c1):
        return [R[k][b] for b, (a0, a1) in enumerate(BLK) if a0 < c1 and c0 < a1]

    def bank(self):
        while True:
            b = self._bank
            self._bank = (b + 1) % 8
            if b not in self._reserved:
                return self.ps[b]

    def bank_reserve(self):
        while True:
            b = self._bank
            self._bank = (b + 1) % 8
            if b not in self._reserved:
                self._reserved.add(b)
                return self.ps[b]

    def bank_release(self, banks):
        for t in banks:
            for i, p in enumerate(self.ps):
                if p.regs[0] is t.regs[0]:
                    self._reserved.discard(i)

    def evac_eng(self):
        self._evac ^= 1
        return "act" if self._evac else "dve"

    def tt(self, eng, out, a, b, op, relax=False):
        self.P.op(eng, lambda e: e.tensor_tensor(out=out.ap, in0=a.ap, in1=b.ap, op=op),
                  reads=a.regs + b.regs, writes=out.regs, relax=relax)

    def ts(self, eng, out, a, s1, s2, op0, op1=None):
        rd = list(a.regs)
        s1a = s1.ap if isinstance(s1, T) else s1
        s2a = s2.ap if isinstance(s2, T) else s2
        if isinstance(s1, T):
            rd += s1.regs
        if isinstance(s2, T):
            rd += s2.regs
        if op1 is None:
            self.P.op(eng, lambda e: e.tensor_scalar(out=out.ap, in0=a.ap, scalar1=s1a, scalar2=None, op0=op0),
                      reads=rd, writes=out.regs)
        else:
            self.P.op(eng, lambda e: e.tensor_scalar(out=out.ap, in0=a.ap, scalar1=s1a, scalar2=s2a, op0=op0, op1=op1),
                      reads=rd, writes=out.regs)

    def stt(self, eng, out, a, scalar, b, op0, op1):
        rd = a.regs + b.regs
        sa = scalar.ap if isinstance(scalar, T) else scalar
        if isinstance(scalar, T):
            rd = rd + scalar.regs
        self.P.op(eng, lambda e: e.scalar_tensor_tensor(out=out.ap, in0=a.ap, scalar=sa, in1=b.ap, op0=op0, op1=op1),
                  reads=rd, writes=out.regs)

    def act(self, out, a, func, scale=1.0, bias=0.0):
        rd = list(a.regs)
        sa = scale.ap if isinstance(scale, T) else scale
        ba = bias.ap if isinstance(bias, T) else bias
        if isinstance(scale, T):
            rd += scale.regs
        if isinstance(bias, T):
            rd += bias.regs
        self.P.op("act", lambda e: e.activation(out=out.ap, in_=a.ap, func=func, bias=ba, scale=sa),
                  reads=rd, writes=out.regs)

    def copy(self, eng, out, a):
        if eng == "act":
            self.act(out, a, AF.Copy)
        else:
            self.P.op(eng, lambda e: e.tensor_copy(out=out.ap, in_=a.ap), reads=a.regs, writes=out.regs)

    def memset(self, eng, out, val):
        self.P.op(eng, lambda e: e.memset(out.ap, val), writes=out.regs)

    def mmgroup(self, out, pairs, start=True, stop=True, extra_reads=()):
        rd = []
        for l, r in pairs:
            rd += l.regs + r.regs
        rd += list(extra_reads)
        n = len(pairs)

        def fn(e):
            ins = None
            for i, (l, r) in enumerate(pairs):
                ins = e.matmul(out.ap, lhsT=l.ap, rhs=r.ap, start=(start and i == 0), stop=(stop and i == n - 1),
                               skip_group_check=True)
            return ins
        self.P.op("pe", fn, reads=rd, writes=out.regs)

    def transpose(self, out, a, ident):
        self.P.op("pe", lambda e: e.transpose(out.ap, a.ap, ident.ap), reads=a.regs + ident.regs, writes=out.regs)

    def dma(self, eng, dsem, pairs, reads=(), writes=(), nonc=False):
        n = len(pairs)

        def fn(e):
            r = []
            for o, i in pairs:
                if nonc:
                    r.append(e.dma_start(out=o, in_=i, allow_slow_non_contiguous=True))
                else:
                    r.append(e.dma_start(out=o, in_=i))
            return r
        return self.P.dma(eng, dsem, fn, reads=list(reads), writes=list(writes), n=n)

    def layout(self):
        self.Rxt = [[Reg("xt%d_%d" % (k, b)) for b in range(5)] for k in range(8)]
        self.Rxn = [[Reg("xn%d_%d" % (k, b)) for b in range(5)] for k in range(8)]
        self.XT = [self.f32(O_XT + k * NT, [NT], regs=self.Rxt[k]) for k in range(8)]
        self.XTb = [[T(self.XT[k].ap[:, c0:c1], [self.Rxt[k][b]]) for b, (c0, c1) in enumerate(BLK)] for k in range(8)]
        self.XTall = T(_rs(self.arena[:, O_XT:O_XT + 8 * NT], [8, NT]), [r for k in range(8) for r in self.Rxt[k]])
        self.XN = [self.bf(O_XN + k * (NT // 2), [NT], regs=self.Rxn[k]) for k in range(8)]
        self.XNb = [[T(self.XN[k].ap[:, c0:c1], [self.Rxn[k][b]]) for b, (c0, c1) in enumerate(BLK)] for k in range(8)]
        self.XNall = T(_rs(self.arena[:, O_XN:O_XN + 4 * NT].bitcast(BF16), [8, NT]), [r for k in range(8) for r in self.Rxn[k]])
        self.slot_regs = [Reg("slot%d" % s) for s in range(NSLOT)]
        self.slot_ds = [self.dsem("slot%d" % s) for s in range(NSLOT)]
        o = O_CONST
        self.ident_f = self.f32(o, [128], name="ident_f"); o += 128
        self.ones_f = self.f32(o, [128], name="ones_f"); o += 128
        self.ident_b = self.bf(o, [128], name="ident_b"); o += 64
        self.ones_b = self.bf(o, [128], name="ones_b"); o += 64
        self.chv = self.f32(o, [8, 32], name="chv"); o += 256
        self.coef = []
        for j in range(2):
            d = {}
            for nm in ("l1r", "l1i", "m7r", "m7i"):
                d[nm] = self.f32(o, [32], name="coef"); o += 32
            d["C4"] = self.f32(o, [32, 4], name="coef"); o += 128
            self.coef.append(d)
        assert o <= AW, o
        self.SQ = [self.bf(O_RB + RB_SQ + h * 2048, [8, 512], name="sq%d" % h) for h in range(2)]
        self.RS = [self.f32(O_RB + RB_RS + h * 512, [512], name="rs%d" % h) for h in range(2)]
        self.LN = self.f32(O_RB + RB_LN, [512], name="ln")
        self.HID = [[self.bf(O_RB + h * 4128 + m * (NT // 2), [NT], name="hid%d_%d" % (h, m)) for m in range(4)]
                    for h in range(2)]
        self.R_sbf, self.R_hn, self.R_psh = Reg("sbf"), Reg("hn"), Reg("psh")
        self.R_hn2 = Reg("hn2")
        self.R_y1s = Reg("y1s")
        self.R_xrot = [Reg("xr0"), Reg("xr1")]
        self.R_grot = [Reg("gr0"), Reg("gr1")]
        self.R_hst = [Reg("hst%d" % i) for i in range(4)]
        self.R_t1, self.R_t2, self.R_ts1, self.R_ts2, self.R_zl, self.R_pst = [Reg("s5t") for _ in range(6)]
        self.R_rstdb = Reg("rstdb")
        self.R_ytmp = [Reg("yt0"), Reg("yt1")]
        self.R_em, self.R_h0, self.R_ss, self.R_hld = Reg("em"), Reg("h0"), Reg("ss"), Reg("hld")
        self.R_sg = [Reg("sg0"), Reg("sg1")]
        self.R_tv = [Reg("tv0"), Reg("tv1")]
        self.R_gated = [Reg("gated%d" % k) for k in range(8)]
        self.R_cv, self.R_yc, self.R_gb = Reg("cv"), Reg("yc"), Reg("gb")
        self.R_gc = [Reg("gc0"), Reg("gc1")]
        self.R_lst, self.R_buf, self.R_cvst = Reg("lst"), Reg("buf"), Reg("cvst")
        self.RSTDB_T = self.bf(O_RB + 13888, [NT], regs=[self.R_rstdb])
        self.d_out = [self.dsem("out0"), self.dsem("out1"), self.dsem("out2")]
        self.P.final_waits.extend(self.d_out)
        self.d_misc = self.dsem("misc")
        self.d_hld = self.dsem("hld")
        self.d_pst = self.dsem("pst")
        self.P.final_waits.extend([self.d_hld, self.d_pst])
        self.d_ost = [self.dsem("ost%d" % q) for q in range(8)]
        self.P.final_waits.extend(self.d_ost)

    def slotview(self, s, shape):
        return self.bf(O_W + s * 2048, shape, regs=[self.slot_regs[s]])

    def consts(self):
        self.memset("pool", self.ones_f, 1.0)
        self.P.op("pool", lambda e: e.affine_select(out=self.ident_f.ap, in_=self.ones_f.ap, pattern=[[1, 128]],
                                                   compare_op=ALU.is_equal, fill=0.0, base=0, channel_multiplier=-1),
                  reads=self.ones_f.regs, writes=self.ident_f.regs)
        self.copy("dve", self.ident_b, self.ident_f)
        self.copy("dve", self.ones_b, self.ones_f)
        self.STG1 = self.f32(O_RB + 14208, [1024], name="stg1")
        st = self.STG1
        self.memset("dve", st, 0.0)
        ds = self.dsem("chst")
        self.dma("sp", ds, [(st.ap[0:4, :], self.norm_mix), (st.ap[4:8, :], self.norm_mlp),
                            (st.ap[8:9, :], self.norm_final), (st.ap[9:15, :], self.cw),
                            (st.ap[15:17, :], self.ssm_d)], writes=st.regs)
        pb = self.bank()
        for k in range(8):
            self.transpose(pb[:, k * 32:(k + 1) * 32], st[0:32, k * 128:(k + 1) * 128], self.ident_f[0:32, 0:32])
        self.copy("act", self.chv, pb[:, 0:256].v(lambda a: a.rearrange("p (k r) -> p k r", k=8)))

    def gcol(self, row, k):
        return self.chv[:, k, row:row + 1]

    def load_inputs(self):
        xpv = self.xp.rearrange("(c i) d -> i c d", i=8)
        stg = self.STG1
        self.dma("sp", self.dsem("sst"), [(stg.ap[0:16, :], self.xs)], writes=stg.regs)
        pb = self.bank()
        for k in range(8):
            self.transpose(pb[:, k * 16:(k + 1) * 16], stg[0:16, k * 128:(k + 1) * 128], self.ident_f[0:16, 0:16])
        xs_out = T(self.XTall.ap[:, :, 0:16], [self.Rxt[k][0] for k in range(8)])
        self.copy("act", xs_out, pb[:, 0:128].v(lambda a: a.rearrange("p (k s) -> p k s", k=8)))
        dsi = self.dsem("in")
        for i in range(8):
            for ch in range(2):
                col = 16 + i * 256 + ch * 128
                self.dma("sp", dsi, [(stg.ap, xpv[i, ch * 128:(ch + 1) * 128, :])], writes=stg.regs)
                for half in range(2):
                    pb = self.bank()
                    for kk in range(4):
                        k = half * 4 + kk
                        self.transpose(pb[:, kk * 128:(kk + 1) * 128], stg[:, k * 128:(k + 1) * 128], self.ident_f)
                    dst = T(self.XTall.ap[:, half * 4:half * 4 + 4, col:col + 128],
                            [r for kk in range(4) for r in self.rcols(self.Rxt, half * 4 + kk, col, col + 128)])
                    self.copy("act", dst, pb.v(lambda a: a.rearrange("p (k c) -> p k c", k=4)))

    def norm_s1(self, bi):
        c0, c1 = BLK[bi]
        w = c1 - c0
        h = bi % 2
        xin = T(self.XTall.ap[:, :, c0:c1], [self.Rxt[k][bi] for k in range(8)])
        self.act(self.SQ[h][:, :, 0:w], xin, AF.Square)

    def norm_s2(self, bi, grow, dst, rstd_full):
        sq, rs, ln = self.SQ, self.RS, self.LN
        c0, c1 = BLK[bi]
        w = c1 - c0
        h = bi % 2
        pb = self.bank()
        self.mmgroup(pb[:, 0:w], [(self.ones_b, sq[h][:, k, 0:w]) for k in range(8)])
        self.act(ln[:, 0:w], pb[:, 0:w], AF.Ln, scale=1.0 / 1024.0, bias=EPS)
        self.act(rs[h][:, 0:w], ln[:, 0:w], AF.Exp, scale=-0.5)
        if rstd_full is not None:
            self.copy("dve", rstd_full[:, c0:c1], rs[h][:, 0:w])
        if dst == "inplace":
            for k in range(8):
                self.stt("dve", self.XTb[k][bi], self.XTb[k][bi], self.gcol(grow, k), rs[h][:, 0:w],
                         ALU.mult, ALU.mult)
        elif dst is not None:
            for k in range(8):
                self.stt("dve", self.XNb[k][bi], self.XTb[k][bi], self.gcol(grow, k), rs[h][:, 0:w],
                         ALU.mult, ALU.mult)

    def norm(self, grow, dst=None, rstd_full=None):
        if self._norm_done == (grow, dst is not None, rstd_full is not None):
            self._norm_done = None
            return
        for bi in range(5):
            self.norm_s1(bi)
            self.norm_s2(bi, grow, dst, rstd_full)

    def fused_norm_hook(self, bi, last):
        nn = self._next_norm
        if nn is None:
            return
        grow, dst, rstd = nn
        self.norm_s1(bi)
        if bi >= 1:
            self.norm_s2(bi - 1, grow, dst, rstd)
        if last:
            self.norm_s2(bi, grow, dst, rstd)
            self._norm_done = (grow, dst is not None, rstd is not None)
            self._next_norm = None

    def wq_build(self):
        q = []
        for L in range(4):
            j = L // 2
            if L % 2 == 0:
                if self.s5:
                    for c in (0, 2, 1, 3):
                        q.append((("glu", j, c), self.w_glu[j].rearrange("(k p) n -> p k n", p=128)[:, :, c * 512:(c + 1) * 512],
                                  [8, 512], None, L))
            else:
                for mq in range(2):
                    for part in (1, 2, 0):
                        c0 = part * 1024 + mq * 512
                        q.append((("cin", j, part, mq), self.cw_in[j].rearrange("(k p) n -> p k n", p=128)[:, :, c0:c0 + 512],
                                  [8, 512], None, L))
                for hh in range(2):
                    q.append((("cout", j, hh), self.cw_out[j].rearrange("(k p) n -> p k n", p=128)[:, :, hh * 512:(hh + 1) * 512],
                              [8, 512], None, L))
            for f in range(8):
                q.append((("up", L, f), self.w_up[L].rearrange("(k p) n -> p k n", p=128)[:, :, f * 512:(f + 1) * 512],
                          [8, 512], None, L))
                q.append((("down", L, f), self.w_down[L].rearrange("(f kk p) n -> f p kk n", f=8, p=128)[f],
                          [4, 1024], None, L))
        q = [x for x in q if x[4] in self.layers]
        self.wq = q
        self.wkey = {x[0]: i for i, x in enumerate(q)}
        self.wnext = 0
        self.wslot = {}
        self._ring = 0
        self.slot_cur = [None] * NSLOT
        self.consumed = set()
        self.gates_closed = set()
        for L in (0, 2):
            if self.s5 and L in self.layers:
                self.gates_closed.add(L)

    def _try_emit(self):
        while self.wnext < len(self.wq):
            key, src, shape, allowed, L = self.wq[self.wnext]
            if L in self.gates_closed:
                break
            s = self._ring if allowed is None else allowed[0]
            cur = self.slot_cur[s]
            if cur is not None and cur not in self.consumed:
                break
            if allowed is None:
                self._ring = (self._ring + 1) % NSLOT
            view = self.slotview(s, shape)
            self.wslot[self.wnext] = view
            self.slot_cur[s] = self.wnext
            self.dma("pool", self.slot_ds[s], [(view.ap, src)], writes=view.regs)
            self.wnext += 1

    def wchunk(self, key):
        idx = self.wkey[key]
        self._try_emit()
        assert idx < self.wnext, (key, idx, self.wnext)
        return self.wslot[idx]

    def wdone(self, key):
        self.consumed.add(self.wkey[key])
        self._try_emit()

    def mlp(self, L):
        self.norm(4 + L, dst=self.XN)
        nl = L + 1
        if L == 3:
            self._next_norm = (8, "inplace", None)
        if nl in self.layers and nl < 4:
            if nl % 2 == 1:
                self._next_norm = (nl, self.XN, None)
            elif self.s5 and self.cut is None:
                self._next_norm = (nl, None, self.RSTDB_T)
        hid = self.HID
        for f in range(8):
            wu = self.wchunk(("up", L, f))
            wd = self.wchunk(("down", L, f))
            hh = hid[f % 2]
            for bi, (c0, c1) in enumerate(BLK):
                w = c1 - c0
                for m in range(4):
                    pb = self.bank()
                    self.mmgroup(pb[:, 0:w], [(wu[:, k, m * 128:(m + 1) * 128], self.XNb[k][bi]) for k in range(8)])
                    self.act(hh[m][:, c0:c1], pb[:, 0:w], AF.Relu)
                    self.tt("dve", hh[m][:, c0:c1], hh[m][:, c0:c1], hh[m][:, c0:c1], ALU.mult)
            for bi, (c0, c1) in enumerate(BLK):
                w = c1 - c0
                for mo in range(8):
                    pb = self.bank()
                    self.mmgroup(pb[:, 0:w], [(wd[:, kk, mo * 128:(mo + 1) * 128], hh[kk][:, c0:c1]) for kk in range(4)])
                    self.tt("dve", self.XTb[mo][bi], pb[:, 0:w], self.XTb[mo][bi], ALU.add)
                if f == 7:
                    self.fused_norm_hook(bi, bi == 4)
            self.wdone(("up", L, f))
            self.wdone(("down", L, f))


    def psbf(self, pb):
        return T(pb.ap.bitcast(BF16), pb.regs)

    def s5prep_a(self, j, first):
        A = lambda off, shape, nm: self.f32(O_RB + off, shape, name=nm)
        AL = A(0, [2, 128], "AL")
        o = [256]

        def tab(nm):
            t = A(o[0], [32], nm)
            o[0] += 32
            return t
        ARE, AIM, DTB = tab("are"), tab("aim"), tab("dtb")
        t1, t2, mag, ang, sn, cs, LR, LI = [tab("t") for _ in range(8)]
        lm1, den, rden, nr, ni, WR, WI, n2, rn = [tab("t") for _ in range(9)]
        assert o[0] <= 1024
        PWr = [A(1024 + 64 * i, [32], "pwr") for i in range(16)]
        PWi = [A(1024 + 64 * i + 32, [32], "pwi") for i in range(16)]
        BRE, BIM = A(2048, [32, 16], "bre"), A(2560, [32, 16], "bim")
        BBR, BBI = A(3072, [32, 16], "bbr"), A(3584, [32, 16], "bbi")
        CRE, CIM = A(4096, [32, 16], "cre"), A(4608, [32, 16], "cim")
        CL = [A(5120, [128], "cl0"), A(5248, [128], "cl1")]
        TA = [self.f32(O_XN, [32, 16], name="ta0")]
        TB = [self.f32(O_XN + 512, [32, 16], regs=TA[0].regs)]
        ET = self.bf(O_RB + 5888, [32, 2, 8, 16], name="ET")
        FT = self.bf(O_RB + 9984, [32, 2, 8, 16], name="FT")
        MASK = A(14080, [128], "mask")
        FF = self.bf(O_W, [32, 2, 8, 16], regs=[self.slot_regs[0], self.slot_regs[1]])
        EM = self.bf(O_W + 4096, [64, 2, 64], regs=[self.slot_regs[2], self.slot_regs[3]])
        TM = self.bf(O_W + 4096, [64, 128], regs=[self.slot_regs[2], self.slot_regs[3]])
        PSHp = self.bf(O_W + 8192, [8, 4, 64], regs=[self.slot_regs[4]])
        dso = self.dsem("prepo")
        D, Pl = "dve", "pool"

        self.dma("sp", self.dsem("pl"), [(AL.ap[0:32, 0, :].rearrange("g (h p) -> g h p", h=2), self.a_re[j].rearrange("(h g) p -> g h p", h=2)),
                            (AL.ap[0:32, 1, :].rearrange("g (h p) -> g h p", h=2), self.a_im[j].rearrange("(h g) p -> g h p", h=2))],
                 writes=AL.regs)
        self.dma("sp", self.dsem("pl"), [(DTB.ap[64 * gh:64 * gh + 64, :], self.log_dt[j:j + 1, 32 * gh:32 * gh + 32].broadcast_to([64, 32]))
                            for gh in range(2)], writes=DTB.regs)
        bv_re = self.b_re[j].rearrange("(h g) p x -> h p g x", h=2)
        bv_im = self.b_im[j].rearrange("(h g) p x -> h p g x", h=2)
        self.dma("sp", self.dsem("pl"), [(BRE.ap[64 * gh:64 * gh + 64], bv_re[gh]) for gh in range(2)], writes=BRE.regs)
        self.dma("sp", self.dsem("pl"), [(BIM.ap[64 * gh:64 * gh + 64], bv_im[gh]) for gh in range(2)], writes=BIM.regs)
        pb = self.bank()
        for ri in range(2):
            self.transpose(pb[:, ri * 32:(ri + 1) * 32], AL[0:32, ri, :], self.ident_f[0:32, 0:32])
        self.copy("act", ARE, pb[:, 0:32])
        self.copy("act", AIM, pb[:, 32:64])
        for ci, (csrc, cdst) in enumerate(((self.c_re, CRE), (self.c_im, CIM))):
            cv = csrc[j].rearrange("(h q g) x p -> q (g x) h p", h=2, q=4)
            cflat = cdst.v(lambda a: a.rearrange("p g x -> p (g x)"))
            for q in range(4):
                cl = CL[q % 2]
                self.dma("sp", self.dsem("pl"), [(cl.ap.rearrange("p (h x) -> p h x", h=2), cv[q])], writes=cl.regs)
                pb = self.bank()
                self.transpose(pb[:, 0:128], cl, self.ident_f)
                self.copy("act", cflat[:, q * 128:(q + 1) * 128], pb[:, 0:128])
        self.act(DTB, DTB, AF.Exp)
        self.tt(D, t1, ARE, DTB, ALU.mult)
        self.act(mag, t1, AF.Exp, scale=1.0 / 64.0)
        self.tt(D, ang, AIM, DTB, ALU.mult)
        self.act(sn, ang, AF.Sin, scale=1.0 / 64.0)
        self.ts(D, t2, ang, 1.0 / 64.0, math.pi / 2.0, ALU.mult, ALU.add)
        self.act(cs, t2, AF.Sin)
        self.tt(D, LR, mag, cs, ALU.mult)
        self.tt(D, LI, mag, sn, ALU.mult)
        for _ in range(6):
            self.tt(D, t1, LR, LR, ALU.mult)
            self.tt(D, t2, LI, LI, ALU.mult)
            self.stt(D, LI, LR, 2.0, LI, ALU.mult, ALU.mult)
            self.tt(D, LR, t1, t2, ALU.subtract)

        def cmul(outr, outi, ar, ai, br, bi):
            self.tt(D, t1, ar, br, ALU.mult)
            self.tt(D, t2, ai, bi, ALU.mult)
            self.tt(D, outr, t1, t2, ALU.subtract)
            self.tt(D, t1, ar, bi, ALU.mult)
            self.tt(D, t2, ai, br, ALU.mult)
            self.tt(D, outi, t1, t2, ALU.add)
        K0 = 7
        self.memset(D, PWr[K0], 1.0)
        self.memset(D, PWi[K0], 0.0)
        self.copy(D, PWr[K0 + 1], LR)
        self.copy(D, PWi[K0 + 1], LI)
        for k in range(2, 9):
            cmul(PWr[K0 + k], PWi[K0 + k], PWr[K0 + k - 1], PWi[K0 + k - 1], LR, LI)
        self.tt(D, t1, LR, LR, ALU.mult)
        self.tt(D, t2, LI, LI, ALU.mult)
        self.tt(D, n2, t1, t2, ALU.add)
        self.P.op(D, lambda e: e.reciprocal(rn.ap, n2.ap), reads=n2.regs, writes=rn.regs)
        self.tt(D, PWr[K0 - 1], LR, rn, ALU.mult)
        self.stt(D, PWi[K0 - 1], LI, -1.0, rn, ALU.mult, ALU.mult)
        for k in range(2, 8):
            cmul(PWr[K0 - k], PWi[K0 - k], PWr[K0 - k + 1], PWi[K0 - k + 1], PWr[K0 - 1], PWi[K0 - 1])
        self.ts(D, lm1, LR, -1.0, None, ALU.add)
        self.tt(D, t1, ARE, ARE, ALU.mult)
        self.tt(D, t2, AIM, AIM, ALU.mult)
        self.tt(D, den, t1, t2, ALU.add)
        self.P.op(D, lambda e: e.reciprocal(rden.ap, den.ap), reads=den.regs, writes=rden.regs)
        self.tt(D, t1, lm1, ARE, ALU.mult)
        self.tt(D, t2, LI, AIM, ALU.mult)
        self.tt(D, nr, t1, t2, ALU.add)
        self.tt(D, t1, LI, ARE, ALU.mult)
        self.tt(D, t2, lm1, AIM, ALU.mult)
        self.tt(D, ni, t1, t2, ALU.subtract)
        self.tt(D, WR, nr, rden, ALU.mult)
        self.tt(D, WI, ni, rden, ALU.mult)
        bc = lambda t: t.v(lambda a: a.unsqueeze(2).broadcast_to([128, 32, 16]))
        self.tt(D, TA[0], bc(WR), BRE, ALU.mult)
        self.tt(D, TB[0], bc(WI), BIM, ALU.mult)
        self.tt(D, BBR, TA[0], TB[0], ALU.subtract)
        self.tt(D, TA[0], bc(WR), BIM, ALU.mult)
        self.tt(D, TB[0], bc(WI), BRE, ALU.mult)
        self.tt(D, BBI, TA[0], TB[0], ALU.add)
        cf = self.coef[j]
        self.copy(D, cf["l1r"], PWr[K0 + 1])
        self.copy(D, cf["l1i"], PWi[K0 + 1])
        self.copy(D, cf["m7r"], PWr[K0 - 7])
        self.copy(D, cf["m7i"], PWi[K0 - 7])
        self.copy(D, cf["C4"][:, :, 0], PWr[K0 + 8])
        self.ts(D, cf["C4"][:, :, 1], PWi[K0 + 8], -1.0, None, ALU.mult)
        self.copy(D, cf["C4"][:, :, 2], PWi[K0 + 8])
        self.copy(D, cf["C4"][:, :, 3], PWr[K0 + 8])
        PWall = self.arena[:, O_RB + 1024:O_RB + 2048].rearrange("p (l r g) -> p l r g", l=16, r=2)
        NPW = A(5376, [16, 32], "npw")
        pw_regs = [x.regs[0] for x in PWr] + [x.regs[0] for x in PWi]
        self.P.op(D, lambda e: e.tensor_scalar(out=NPW.ap, in0=PWall[:, :, 1, :], scalar1=-1.0, scalar2=None, op0=ALU.mult),
                  reads=pw_regs, writes=NPW.regs)

        def lv(ap3):
            return ap3.rearrange("p i g -> p g i").unsqueeze(3).broadcast_to([128, 32, 8, 16])
        b4 = lambda t: t.v(lambda a: a.unsqueeze(2).broadcast_to([128, 32, 8, 16]))
        TAd = self.f32(O_XN, [32, 8, 16], regs=TA[0].regs)
        TBd = self.f32(O_XN + 4096, [32, 8, 16], name="TBd")
        Pr_e = T(lv(PWall[:, 14:6:-1, 0, :]), pw_regs)
        Pi_e = T(lv(PWall[:, 14:6:-1, 1, :]), pw_regs)
        self.tt(D, TAd, Pr_e, b4(BBR), ALU.mult)
        self.tt(D, TBd, Pi_e, b4(BBI), ALU.mult)
        self.tt(D, ET[:, :, 0], TAd, TBd, ALU.subtract)
        self.tt(D, TAd, Pr_e, b4(BBI), ALU.mult)
        self.tt(D, TBd, Pi_e, b4(BBR), ALU.mult)
        self.tt(D, ET[:, :, 1], TAd, TBd, ALU.add)
        CQ = self.bf(O_RB + 9984, [32, 2, 16], name="CQ")
        self.copy(D, CQ[:, :, 0, :], CRE)
        self.ts(D, CQ[:, :, 1, :], CIM, -1.0, None, ALU.mult)
        for (dst, l0) in ((FF, 8),):
            Pr_f = T(lv(PWall[:, l0:l0 + 8, 0, :]), pw_regs)
            Pi_f = T(lv(PWall[:, l0:l0 + 8, 1, :]), pw_regs)
            NPi_f = T(lv(NPW.ap[:, l0:l0 + 8, :]), NPW.regs)
            self.tt(D, TAd, b4(CRE), Pr_f, ALU.mult)
            self.tt(D, TBd, b4(CIM), Pi_f, ALU.mult)
            self.tt(D, dst[:, :, 0], TAd, TBd, ALU.subtract)
            self.tt(D, TAd, b4(CRE), NPi_f, ALU.mult)
            self.tt(D, TBd, b4(CIM), Pr_f, ALU.mult)
            self.tt(D, dst[:, :, 1], TAd, TBd, ALU.subtract)
        if first:
            SEL = self.bf(O_RB + 11008, [8, 8, 16], name="SEL")
            for jj in range(8):
                self.P.op(Pl, (lambda jj: (lambda e: e.affine_select(
                    out=SEL.ap[:, jj], in_=self.ones_f.ap.rearrange("p (a b) -> p a b", a=8), pattern=[[-16, 8], [-1, 16]],
                    compare_op=ALU.is_equal, fill=0.0, base=-16 * (7 - jj), channel_multiplier=1)))(jj),
                    reads=self.ones_f.regs, writes=SEL.regs)
            self._sel = SEL
            self.memset(Pl, PSHp, 0.0)
            for a in range(8):
                for b4 in range(4):
                    dst = PSHp[:, a, b4, 16 * b4:16 * b4 + 16]
                    self.P.op(Pl, (lambda dst, a: (lambda e: e.affine_select(
                        out=dst.ap, in_=self.ones_f.ap[:, 0:16], pattern=[[1, 16]], compare_op=ALU.is_equal,
                        fill=0.0, base=16 * a, channel_multiplier=-1)))(dst, a),
                        reads=self.ones_f.regs, writes=dst.regs)
            self.dma("sp", self.dsem("pstP"), [(self.scrP, PSHp.ap.rearrange("p a b m -> p (a b m)"))], reads=PSHp.regs)
        return dict(j=j, ET=ET, CQ=CQ, FF=FF, EM=EM, TM=TM, dso=dso)

    def s5prep_b(self, c):
        j, ET, CQ, FF, EM, TM, dso = c['j'], c['ET'], c['CQ'], c['FF'], c['EM'], c['TM'], c['dso']
        D = 'dve'
        SEL = self._sel
        ETf = ET.v(lambda a: a.rearrange("p g r i x -> p g r (i x)"))
        EMf = EM.v(lambda a: a.rearrange("p g r m -> p (g r m)"))
        for b8 in range(8):
            pb = self.bank()
            pbb = self.psbf(pb)
            for gl in range(8):
                g = b8 * 8 + gl
                gh, gp = g // 32, g % 32
                for ri in range(2):
                    sl = gl * 2 + ri
                    self.transpose(pbb[:, sl * 64:(sl + 1) * 64], ETf[64 * gh:64 * gh + 64, gp, ri, :],
                                   self.ident_b[64 * gh:64 * gh + 64, 64 * gh:64 * gh + 64])
            self.copy(self.evac_eng(), EMf[:, b8 * 1024:(b8 + 1) * 1024], pbb[:, 0:1024])
        self.dma("sp", self.dsem("pstE"), [(self.scrE[j], EMf.ap)], reads=EM.regs)
        KM = self.bf(O_RB + 10496, [64, 16], name="KM")
        for half in range(2):
            pb = self.bank()
            for gp in range(32):
                g = half * 32 + gp
                self.mmgroup(pb[:, gp * 16:(gp + 1) * 16],
                             [(ETf[64 * half:64 * half + 64, gp, ri, :], CQ[64 * half:64 * half + 64, gp, ri, :]) for ri in range(2)])
            self.copy(self.evac_eng(), KM[:, half * 32:(half + 1) * 32, :], pb.v(lambda a: a.rearrange("p (g x) -> p g x", x=16)))
        for b16 in range(16):
            pb = self.bank()
            for gl in range(4):
                g = b16 * 4 + gl
                for jj in range(8):
                    self.mmgroup(pb[:, gl * 128 + jj * 16:gl * 128 + (jj + 1) * 16],
                                 [(SEL[:, jj].v(lambda a: a.rearrange("p a b -> p (a b)")), KM[:, g, :])],
                                 start=(jj == 0), stop=(jj == 7))
            self.copy(self.evac_eng(), TM[:, b16 * 4:(b16 + 1) * 4, :], pb.v(lambda a: a.rearrange("p (g m) -> p g m", g=4)))
        self.dma("sp", dso, [(self.scrT[j], TM.ap.rearrange("p g m -> p (g m)")),
                             (self.scrF[j], FF.ap.rearrange("p g r i x -> p (g r i x)"))],
                 reads=TM.regs + FF.regs)
        self.P.barrier()

    def _s5_abort(self, L):
        for s in range(4):
            self.slot_cur[s] = None
        self.gates_closed.discard(L)
        for kk in [x[0] for x in self.wq if x[0][0] == "glu" and x[4] == L]:
            self.wchunk(kk)
            self.wdone(kk)
        self.bank_release(list(self.ps))
        self.P.barrier()

    def s5mix(self, L):
        j = L // 2
        cf = self.coef[j]
        RB = O_RB
        SBF = self.bf(RB, [32, 2, 273], regs=[self.R_sbf])
        HNs = [self.bf(RB + 8736, [NT], regs=[self.R_hn]), self.bf(RB + 11720, [NT], regs=[self.R_hn2])]
        PSH = self.bf(RB + 9768, [8, 4, 64], regs=[self.R_psh])
        XROT = [self.bf(RB + 10792 + h * 136, [272], regs=[self.R_xrot[h]]) for h in range(2)]
        GROT = [self.bf(RB + 11064 + h * 136, [272], regs=[self.R_grot[h]]) for h in range(2)]
        HST = [self.f32(RB + 11336 + h * 64, [32, 2], regs=[self.R_hst[h]]) for h in range(4)]
        T1 = self.f32(RB + 11592, [32, 2], regs=[self.R_t1])
        T2 = self.f32(RB + 11656, [32, 2], regs=[self.R_t2])
        TS1 = self.f32(RB + 11720, [32, 16], regs=[self.R_ts1])
        TS2 = self.f32(RB + 12232, [32, 16], regs=[self.R_ts2])
        ZL = self.bf(RB + 13064, [64], regs=[self.R_zl])
        PST = self.f32(RB + 12808, [256], regs=[self.R_pst])
        RSTDB = self.RSTDB_T
        YTMP = [self.f32(RB + 14920 + h * 512, [512], regs=[self.R_ytmp[h]]) for h in range(2)]
        EM = self.bf(O_XN, [64, 2, 64], regs=[self.R_em] + [r for k in range(0, 4) for r in self.Rxn[k]])
        H0 = self.f32(O_XN + 4096, [32, 2, 16], regs=[self.R_h0] + self.Rxn[3] + self.Rxn[4])
        SS = self.f32(O_XN + 5120, [2, 32, 16], regs=[self.R_ss] + self.Rxn[4] + self.Rxn[5])
        HLD = self.f32(O_XN + 6144, [2048], regs=[self.R_hld] + self.Rxn[5] + self.Rxn[6] + self.Rxn[7])
        TM = self.bf(O_W, [64, 128], regs=[self.slot_regs[0], self.slot_regs[1]])
        FF = self.bf(O_W + 4096, [32, 2, 128], regs=[self.slot_regs[2], self.slot_regs[3]])
        for s in range(4):
            self.slot_cur[s] = "s5"
        dsl = [self.dsem("s5l%d" % i) for i in range(6)]
        self.dma("sp", dsl[2], [(EM.ap.rearrange("p g r m -> p (g r m)"), self.scrE[j])], writes=EM.regs)
        self.dma("sp", dsl[0], [(TM.ap.rearrange("p g m -> p (g m)"), self.scrT[j])], writes=TM.regs)
        self.dma("sp", dsl[1], [(FF.ap.rearrange("p g r m -> p (g r m)"), self.scrF[j])], writes=FF.regs)
        self.norm(L, dst=None, rstd_full=RSTDB)
        self.P.barrier()
        self.dma("sp", dsl[3], [(PSH.ap.rearrange("p a b m -> p (a b m)"), self.scrP)], writes=PSH.regs)
        self.memset("dve", ZL, 0.0)
        self.memset("dve", SBF[:, :, :, 16:17], 0.0)
        for ri, src in enumerate((self.sre, self.sim)):
            sv = src[j].rearrange("s (h q r p) -> r s h q p", h=2, q=16, r=2)
            self.dma("sp", dsl[4 + ri], [(HLD.ap[16 * r:16 * r + 16, :].rearrange("s (q h p) -> s h q p", h=2, q=16)[:, h], sv[r][:, h])
                                         for r in range(2) for h in range(2)], writes=HLD.regs)
            pb = self.bank()
            for q in range(16):
                self.transpose(pb[:, q * 32:(q + 1) * 32], HLD[0:32, q * 128:(q + 1) * 128], self.ident_f[0:32, 0:32])
            self.copy("act", H0[:, :, ri, :], pb.v(lambda a: a.rearrange("p (g s) -> p g s", s=16)))

        if self.cut == "A0":
            return self._s5_abort(L)

        def hn_build(k, pool=False):
            HN = HNs[k % 2]
            if pool:
                self.ts("pool", HN, self.XT[k], self.gcol(L, k), None, ALU.mult)
                self.tt("pool", HN, HN, RSTDB, ALU.mult)
            else:
                self.stt("dve", HN, self.XT[k], self.gcol(L, k), RSTDB, ALU.mult, ALU.mult)
            return HN

        def x_build(HN, gl, g, eng=None):
            pX = self.bank()
            self.mmgroup(pX[0:64, 0:16], [(ZL, HN[:, 0:16])])
            self.mmgroup(pX[64:128, 0:16], [(PSH[:, gl, 3, :], HN[:, 0:16])])
            for half in range(2):
                self.mmgroup(pX[64 * half:64 * half + 64, 16:272],
                             [(PSH[:, gl, b4, :], HN[:, 16 + 256 * (4 * half + b4):16 + 256 * (4 * half + b4 + 1)])
                              for b4 in range(4)])
            xr = XROT[g % 2]
            self.copy(eng or self.evac_eng(), xr, pX[:, 0:272])
            return xr

        pSS = [self.bank_reserve(), self.bank_reserve()]

        def s_part(g, xr):
            gh, gp = g // 32, g % 32
            pS = self.bank()
            for ri in range(2):
                self.mmgroup(pS[64 * gh:64 * gh + 64, ri * 256:(ri + 1) * 256], [(EM[:, g, ri, :], xr[:, 16:272])],
                             start=(ri == 0), stop=(ri == 1))
                self.mmgroup(pSS[ri][64 * gh:64 * gh + 64, gp * 16:(gp + 1) * 16], [(EM[:, g, ri, :], xr[:, 0:16])])
            self.copy(self.evac_eng(), SBF[64 * gh:64 * gh + 64, gp, :, 17:273],
                      pS[64 * gh:64 * gh + 64, :].v(lambda a: a.rearrange("p (r c) -> p r c", r=2)))
        pend = None
        for k in range(8):
            HN = hn_build(k)
            for gl in range(8):
                g = 8 * k + gl
                xr = x_build(HN, gl, g)
                if pend is not None:
                    s_part(*pend)
                pend = (g, xr)
        s_part(*pend)
        for ri in range(2):
            self.copy(self.evac_eng(), SS[:, ri, :, :], pSS[ri].v(lambda a: a.rearrange("p (g s) -> p g s", s=16)))
        self.bank_release(pSS)
        self.P.barrier()
        if self.cut == "A":
            return self._s5_abort(L)
        def sample_math():
            bc = lambda t: t.v(lambda a: a.unsqueeze(2).broadcast_to([128, 32, 16]))
            h0r, h0i = H0[:, :, 0, :], H0[:, :, 1, :]
            D = "dve"
            TS1 = T(YTMP[0].ap.rearrange("p (g s) -> p g s", s=16), YTMP[0].regs)
            TS2 = T(YTMP[1].ap.rearrange("p (g s) -> p g s", s=16), YTMP[1].regs)
            self.tt(D, TS1, bc(cf["m7r"]), h0r, ALU.mult)
            self.tt(D, TS2, bc(cf["m7i"]), h0i, ALU.mult)
            self.tt(D, SBF[:, :, 0, 0:16], TS1, TS2, ALU.subtract)
            self.tt(D, TS1, bc(cf["m7r"]), h0i, ALU.mult)
            self.tt(D, TS2, bc(cf["m7i"]), h0r, ALU.mult)
            self.tt(D, SBF[:, :, 1, 0:16], TS1, TS2, ALU.add)
            self.tt(D, TS1, bc(cf["l1r"]), h0r, ALU.mult)
            self.tt(D, SS[:, 0, :, :], SS[:, 0, :, :], TS1, ALU.add)
            self.tt(D, TS2, bc(cf["l1i"]), h0i, ALU.mult)
            self.tt(D, SS[:, 0, :, :], SS[:, 0, :, :], TS2, ALU.subtract)
            self.tt(D, TS1, bc(cf["l1r"]), h0i, ALU.mult)
            self.tt(D, SS[:, 1, :, :], SS[:, 1, :, :], TS1, ALU.add)
            self.tt(D, TS2, bc(cf["l1i"]), h0r, ALU.mult)
            self.tt(D, SS[:, 1, :, :], SS[:, 1, :, :], TS2, ALU.add)

        def sample_out(ri):
            dst = (self.nre_s, self.nim_s)[ri]
            pbs = [self.bank() for _ in range(4)]
            for q in range(16):
                self.transpose(pbs[q // 4][0:32, (q % 4) * 128:(q % 4 + 1) * 128],
                               SS[:, ri, 2 * q:2 * q + 2, :].v(lambda a: a.rearrange('p g s -> p (g s)')), self.ident_f)
            for b in range(4):
                self.copy("act", HLD[0:32, b * 512:(b + 1) * 512], pbs[b][0:32, :])
            dv = dst[j].rearrange("s (h q r p) -> r s h q p", h=2, q=16, r=2)
            self.dma("sp", self.d_hld, [(dv[r][:, h], HLD.ap[16 * r:16 * r + 16, :].rearrange("s (q h p) -> s h q p", h=2, q=16)[:, h])
                                           for r in range(2) for h in range(2)], reads=HLD.regs)
        if self.cut == "S":
            return self._s5_abort(L)
        D = "dve"
        self.P.barrier()
        Rcol = [Reg("sbfc%d" % c) for c in range(256)]
        Rh = [[Reg("hsth") for _ in range(2)] for _ in range(4)]
        Rt = [[Reg("st1"), Reg("st1")], [Reg("st2"), Reg("st2")]]
        hs = lambda i, hf: T(HST[i].ap[:, 16 * hf:16 * hf + 16, :], [Rh[i][hf]])
        t1 = lambda hf: T(T1.ap[:, 16 * hf:16 * hf + 16, :], [Rt[0][hf]])
        t2 = lambda hf: T(T2.ap[:, 16 * hf:16 * hf + 16, :], [Rt[1][hf]])
        P4 = self.f32(RB + 11592, [32, 2, 2], regs=[Rt[1][0]])
        p4 = lambda hf: T(P4.ap[:, 16 * hf:16 * hf + 16], [Rt[1][hf]])
        c4 = lambda hf: cf["C4"][:, 16 * hf:16 * hf + 16, :].v(lambda a: a.rearrange("p g (x y) -> p g x y", x=2))
        T1s = self.f32(RB + 12752, [32, 2], regs=[Rt[0][0]])
        t1 = lambda hf: T(T1s.ap[:, 16 * hf:16 * hf + 16, :], [Rt[0][hf]])
        col = lambda c, hf: T(SBF.ap[:, 16 * hf:16 * hf + 16, :, 17 + c], [Rcol[c]])
        for hf in range(2):
            self.copy("dve", hs(0, hf), col(0, hf))
        Y1S = self.bf(RB + 13128, [64, 16], regs=[self.R_y1s])

        def y1_part(g, xr):
            k, gl = g // 8, g % 8
            pY1 = self.bank()
            self.mmgroup(pY1[:, 0:272], [(TM[:, g, :], xr)])
            self.copy("act", Y1S[:, g, :], pY1[:, 0:16])
            self.copy("act", T(self.XN[k].ap[:, 16 + 256 * gl:16 + 256 * (gl + 1)], self.Rxn[k]), pY1[:, 16:272])
        ypend = None
        HNc = None
        for c in range(1, 256):
            if c == 2:
                sample_math()
            if c == 6:
                sample_out(0)
            if c == 30:
                sample_out(1)
            if c % 4 == 1:
                g = (c - 1) // 4
                if g % 8 == 0:
                    HNc = hn_build(g // 8)
                xr = x_build(HNc, g % 8, g, eng="act")
                if ypend is not None:
                    y1_part(*ypend)
                ypend = (g, xr)
            ip, ic = (c - 1) % 4, c % 4
            for hf in range(2):
                pv = hs(ip, hf).v(lambda a: a.unsqueeze(2).broadcast_to([128, 16, 2, 2]))
                self.tt(D, p4(hf), c4(hf), pv, ALU.mult, relax=True)
            for hf in range(2):
                self.tt(D, t1(hf), p4(hf).v(lambda a: a[:, :, :, 0]), p4(hf).v(lambda a: a[:, :, :, 1]), ALU.add, relax=True)
            for hf in range(2):
                self.tt(D, hs(ic, hf), t1(hf), col(c, hf), ALU.add, relax=True)
            self.copy("act", T(SBF.ap[:, :, :, 17 + c], [Rcol[c]]), T(HST[ic].ap, [Rh[ic][0], Rh[ic][1]]))
        y1_part(*ypend)
        last = T(HST[255 % 4].ap, [Rh[255 % 4][0], Rh[255 % 4][1]])
        pb = self.bank()
        for ri in range(2):
            self.transpose(pb[0:32, ri * 128:(ri + 1) * 128], last[:, :, ri], self.ident_f)
        self.copy("act", PST[0:32, :], pb[0:32, 0:256])
        self.dma("sp", self.d_pst, [(self.nre_p[j].rearrange("(h g p) -> g h p", h=2, g=32), PST.ap[0:32, 0:128].rearrange("g (h p) -> g h p", h=2)),
                                       (self.nim_p[j].rearrange("(h g p) -> g h p", h=2, g=32), PST.ap[0:32, 128:256].rearrange("g (h p) -> g h p", h=2))],
                 reads=PST.regs)
        self.P.barrier()
        if self.cut == "scan":
            return self._s5_abort(L)
        def y_part(g, gr):
            gh, gp = g // 32, g % 32
            k, gl = g // 8, g % 8
            pY = self.bank()
            self.mmgroup(pY[:, 0:16], [(self.ident_b, Y1S[:, g, :])], start=True, stop=False)
            self.mmgroup(pY[:, 16:272], [(self.ident_b, T(self.XN[k].ap[:, 16 + 256 * gl:16 + 256 * (gl + 1)], self.Rxn[k]))],
                         start=False, stop=False)
            self.mmgroup(pY[:, 0:272], [(FF[64 * gh:64 * gh + 64, gp, 0, :], SBF[64 * gh:64 * gh + 64, gp, 0, 0:272]),
                                        (FF[64 * gh:64 * gh + 64, gp, 1, :], SBF[64 * gh:64 * gh + 64, gp, 1, 0:272])],
                         start=False, stop=True)
            self.copy(self.evac_eng(), gr, pY[:, 0:272])
            return gr

        GR4 = GROT + [T(XROT[h].ap, XROT[h].regs) for h in range(2)]

        def u_pair(ga, gra, gb, grb, yP, yS):
            for (g, gr) in ((ga, gra), (gb, grb)):
                gl = g % 8
                half, b4 = gl // 4, gl % 4
                self.mmgroup(yS[64 * half:64 * half + 64, 0:16], [(PSH[:, 7, b4, :], gr[:, 0:16])],
                             start=(b4 == 0), stop=(b4 == 3))
            for jj in range(8):
                for (g, gr) in ((ga, gra), (gb, grb)):
                    gl = g % 8
                    half, b4 = gl // 4, gl % 4
                    self.mmgroup(yP[jj // 2][64 * half:64 * half + 64, (jj % 2) * 256:(jj % 2) * 256 + 256],
                                 [(PSH[:, jj, b4, :], gr[:, 16:272])],
                                 start=(b4 == 0 and jj % 2 == 0), stop=(b4 == 3 and jj % 2 == 1))
        npair = 0
        for k in range(8):
            HN = hn_build(k)
            yP = [self.bank_reserve() for _ in range(4)]
            yS = self.bank_reserve()
            pp = None
            for b4 in range(4):
                ga, gb = 8 * k + b4, 8 * k + 4 + b4
                gra = y_part(ga, GR4[(2 * npair) % 4])
                grb = y_part(gb, GR4[(2 * npair + 1) % 4])
                npair += 1
                if pp is not None:
                    u_pair(*pp, yP, yS)
                pp = (ga, gra, gb, grb)
            u_pair(*pp, yP, yS)
            dk = self.gcol(15 + j, k)
            self.stt("dve", YTMP[0][:, 0:16], HN[:, 0:16], dk, yS[:, 0:16], ALU.mult, ALU.add)
            self.act(T(self.XN[k].ap[:, 0:16], self.rcols(self.Rxn, k, 0, 16)), YTMP[0][:, 0:16], AF.Gelu_apprx_tanh)
            for b in range(4):
                c0 = 16 + 512 * b
                yt = YTMP[(b + 1) % 2]
                self.stt("dve", yt, HN[:, c0:c0 + 512], dk, yP[b], ALU.mult, ALU.add)
                self.act(T(self.XN[k].ap[:, c0:c0 + 512], self.rcols(self.Rxn, k, c0, c0 + 512)), yt, AF.Gelu_apprx_tanh)
            self.bank_release(yP + [yS])
        for s in range(4):
            self.slot_cur[s] = None
        self.gates_closed.discard(L)
        self.P.barrier()
        self._next_norm = (4 + L, self.XN, None)
        SG = [self.f32(RB + 13888 + h * 512, [512], regs=[self.R_sg[h]]) for h in range(2)]
        TV = [self.f32(RB + 13888 + 1024 + h * 512, [512], regs=[self.R_tv[h]]) for h in range(2)]
        n = 0
        for half in range(2):
            wv = self.wchunk(("glu", j, half))
            wg = self.wchunk(("glu", j, 2 + half))
            for bi, (c0, c1) in enumerate(BLK):
                w = c1 - c0
                for ml in range(4):
                    m = half * 4 + ml
                    h = n % 2
                    n += 1
                    pv = self.bank()
                    self.mmgroup(pv[:, 0:w], [(wv[:, kk, ml * 128:(ml + 1) * 128], self.XNb[kk][bi]) for kk in range(8)])
                    pg = self.bank()
                    self.mmgroup(pg[:, 0:w], [(wg[:, kk, ml * 128:(ml + 1) * 128], self.XNb[kk][bi]) for kk in range(8)])
                    self.act(SG[h][:, 0:w], pg[:, 0:w], AF.Sigmoid)
                    self.tt("dve", TV[h][:, 0:w], pv[:, 0:w], SG[h][:, 0:w], ALU.mult)
                    self.tt("dve", self.XTb[m][bi], TV[h][:, 0:w], self.XTb[m][bi], ALU.add)
                if half == 1:
                    self.fused_norm_hook(bi, bi == 4)
            self.wdone(("glu", j, half))
            self.wdone(("glu", j, 2 + half))

    def conv(self, L):
        j = L // 2
        self.norm(L, dst=self.XN)
        self._next_norm = (4 + L, self.XN, None)
        self.P.barrier()
        GATED = [self.bf(O_RB + k * (NT // 2), [NT], regs=self.HID[k // 4][k % 4].regs) for k in range(8)]
        CV = self.f32(O_RB + 8256, [NT], regs=[self.R_cv])
        YC = self.f32(O_RB + 10320, [NT], regs=[self.R_yc])
        GB = self.bf(O_RB + 12384, [NT], regs=[self.R_gb])
        GC = [self.f32(O_RB + 13416 + h * 512, [512], regs=[self.R_gc[h]]) for h in range(2)]
        LST = self.f32(O_RB + 14440, [1024], regs=[self.R_lst])
        BUF = self.f32(O_RB + 15464, [8, 32], regs=[self.R_buf])
        CVST = self.f32(O_RB + 15720, [8, 18], regs=[self.R_cvst])
        self.dma("sp", self.dsem("lst"), [(LST.ap[0:32, :], self.scv[j].rearrange("s r d -> (s r) d"))], writes=LST.regs)
        pb = self.bank()
        for k in range(8):
            self.transpose(pb[:, k * 32:(k + 1) * 32], LST[0:32, k * 128:(k + 1) * 128], self.ident_f[0:32, 0:32])
        self.copy("act", BUF, pb[:, 0:256].v(lambda a: a.rearrange("p (k r) -> p k r", k=8)))
        r0 = 9 + 3 * j
        P1 = 16 + 1792
        P2 = 16 + 1536
        for mq in range(2):
            wgc = self.wchunk(("cin", j, 1, mq))
            wv = self.wchunk(("cin", j, 2, mq))
            wgb = self.wchunk(("cin", j, 0, mq))
            for ml in range(4):
                m = mq * 4 + ml
                for bi, (c0, c1) in enumerate(BLK):
                    w = c1 - c0
                    h = bi % 2
                    pgc = self.bank()
                    self.mmgroup(pgc[:, 0:w], [(wgc[:, k, ml * 128:(ml + 1) * 128], self.XNb[k][bi]) for k in range(8)])
                    pv = self.bank()
                    self.mmgroup(pv[:, 0:w], [(wv[:, k, ml * 128:(ml + 1) * 128], self.XNb[k][bi]) for k in range(8)])
                    pgb = self.bank()
                    self.mmgroup(pgb[:, 0:w], [(wgb[:, k, ml * 128:(ml + 1) * 128], self.XNb[k][bi]) for k in range(8)])
                    self.copy("act", GC[h][:, 0:w], pgc[:, 0:w])
                    self.tt("dve", CV[:, c0:c1], pv[:, 0:w], GC[h][:, 0:w], ALU.mult)
                    self.copy("act", GB[:, c0:c1], pgb[:, 0:w])
                w0 = self.gcol(r0 + 0, m)
                w1 = self.gcol(r0 + 1, m)
                w2 = self.gcol(r0 + 2, m)
                self.ts("dve", YC, CV, w2, None, ALU.mult)
                self.stt("dve", YC[:, 272:NT], CV[:, 16:16 + 1792], w1, YC[:, 272:NT], ALU.mult, ALU.add)
                self.stt("dve", YC[:, 17:272], CV[:, P1:P1 + 255], w1, YC[:, 17:272], ALU.mult, ALU.add)
                self.stt("dve", YC[:, 528:NT], CV[:, 16:16 + 1536], w0, YC[:, 528:NT], ALU.mult, ALU.add)
                self.stt("dve", YC[:, 17:272], CV[:, P2:P2 + 255], w0, YC[:, 17:272], ALU.mult, ALU.add)
                self.stt("dve", YC[:, 273:528], CV[:, P1:P1 + 255], w0, YC[:, 273:528], ALU.mult, ALU.add)
                bk = BUF[:, m, :].v(lambda a: a.rearrange("p (s r) -> p s r", r=2))
                self.stt("dve", YC[:, 0:16], bk[:, :, 1], w1, YC[:, 0:16], ALU.mult, ALU.add)
                self.stt("dve", YC[:, 0:16], bk[:, :, 0], w0, YC[:, 0:16], ALU.mult, ALU.add)
                self.tt("dve", GATED[m], GB, YC, ALU.mult)
                self.copy("act", CVST[:, m, 0:16], CV[:, 0:16])
                self.copy("act", CVST[:, m, 16:17], CV[:, P2 + 255:P2 + 256])
                self.copy("act", CVST[:, m, 17:18], CV[:, P1 + 255:P1 + 256])
            self.wdone(("cin", j, 1, mq))
            self.wdone(("cin", j, 2, mq))
            self.wdone(("cin", j, 0, mq))
        for hh in range(2):
            wo = self.wchunk(("cout", j, hh))
            for bi, (c0, c1) in enumerate(BLK):
                w = c1 - c0
                for ml in range(4):
                    mo = hh * 4 + ml
                    pb = self.bank()
                    self.mmgroup(pb[:, 0:w], [(wo[:, k, ml * 128:(ml + 1) * 128], GATED[k][:, c0:c1]) for k in range(8)])
                    self.tt("dve", self.XTb[mo][bi], pb[:, 0:w], self.XTb[mo][bi], ALU.add)
                if hh == 1:
                    self.fused_norm_hook(bi, bi == 4)
            self.wdone(("cout", j, hh))
        for half in range(2):
            pb = self.bank()
            for kk in range(4):
                k = half * 4 + kk
                self.transpose(pb[0:18, kk * 128:(kk + 1) * 128], CVST[:, k, :], self.ident_f)
            self.copy("act", LST[0:18, half * 512:(half + 1) * 512], pb[0:18, 0:512])
        self.dma("sp", self.d_out[0], [(self.ncv_s[j][:, 1, :], LST.ap[0:16, :]),
                                       (self.ncv_p[j], LST.ap[16:18, :]),
                                       (self.ncv_s[j][:, 0, :], self.scv[j][:, 1, :])], reads=LST.regs)

    def final_out(self):
        self.P.barrier()
        if self._norm_done == (8, True, False):
            self._norm_done = None
        else:
            rsb = self.f32(O_RB + 4608, [NT], name="rstd_fin")
            sq = [self.bf(O_RB + h * 2048, [8, 512], name="sqf%d" % h) for h in range(2)]
            ln = self.f32(O_RB + 4096, [512], name="lnf")
            for bi, (c0, c1) in enumerate(BLK):
                w = c1 - c0
                h = bi % 2
                self.act(sq[h][:, :, 0:w], self.XTall[:, :, c0:c1], AF.Square)
                pb = self.bank()
                self.mmgroup(pb[:, 0:w], [(self.ones_b, sq[h][:, k, 0:w]) for k in range(8)])
                self.act(ln[:, 0:w], pb[:, 0:w], AF.Ln, scale=1.0 / 1024.0, bias=EPS)
                self.act(rsb[:, c0:c1], ln[:, 0:w], AF.Exp, scale=-0.5)
            for k in range(8):
                self.stt("dve", self.XT[k], self.XT[k], self.gcol(8, k), rsb, ALU.mult, ALU.mult)
        if self.dbg:
            self.dma("sp", self.d_out[2], [(self.dbg_xt, self.arena[:, O_XT:O_XT + 8 * NT])], reads=self.XTall.regs)
        ost = [self.f32(O_XN + q * 1024, [1024], name="ost%d" % q) for q in range(8)]
        ypv = self.y_p.rearrange("(c i) d -> i c d", i=8)
        sst = self.f32(O_W, [1024], name="osts")
        pb = self.bank()
        for k in range(4):
            self.transpose(pb[0:16, k * 128:(k + 1) * 128], self.XT[k][:, 0:16], self.ident_f)
        self.copy("act", sst[0:16, 0:512], pb[0:16, 0:512])
        pb2 = self.bank()
        for k in range(4, 8):
            self.transpose(pb2[0:16, (k - 4) * 128:(k - 3) * 128], self.XT[k][:, 0:16], self.ident_f)
        self.copy("dve", sst[0:16, 512:1024], pb2[0:16, 0:512])
        self.dma("sp", self.d_out[2], [(self.y_s, sst.ap[0:16, :])], reads=sst.regs)
        n = 0
        for i in range(8):
            for ch in range(2):
                q = n % 8
                n += 1
                col = 16 + i * 256 + ch * 128
                for half in range(2):
                    pb = self.bank()
                    for kk in range(4):
                        k = half * 4 + kk
                        self.transpose(pb[:, kk * 128:(kk + 1) * 128], self.XT[k][:, col:col + 128], self.ident_f)
                    self.copy(self.evac_eng(), ost[q][:, half * 512:(half + 1) * 512], pb)
                self.dma("sp", self.d_ost[q], [(ypv[i, ch * 128:(ch + 1) * 128, :], ost[q].ap)], reads=ost[q].regs)

    def build(self):
        with self.es:
            self.declare()
            self.layout()
            self.wq_build()
            self.consts()
            ctxs = []
            if self.s5:
                s5l = [L for L in (0, 2) if L in self.layers]
                if s5l:
                    ctxs.append(self.s5prep_a(s5l[0] // 2, True))
            self.load_inputs()
            if self.s5:
                if s5l:
                    self.s5prep_b(ctxs[0])
                for L in s5l[1:]:
                    self.s5prep_b(self.s5prep_a(L // 2, False))
            self.P.barrier()
            for L in range(4):
                if L not in self.layers:
                    continue
                if L % 2 == 1:
                    self.conv(L)
                elif self.s5 and self.cut != "prep":
                    self.s5mix(L)
                elif self.s5:
                    self.gates_closed.discard(L)
                    for kk in [x[0] for x in self.wq if x[0][0] == "glu" and x[4] == L]:
                        self.wchunk(kk)
                        self.wdone(kk)
                self.mlp(L)
            self.final_out()
            self.P.emit(self.sems)
        return self.nc


_IN_KEYS = ["norm_mix", "norm_mlp", "a_re", "a_im", "log_dt", "b_re", "b_im", "c_re", "c_im", "ssm_d",
            "w_glu", "cw_in", "cw_out", "w_up", "w_down"]


def make_in_maps(inputs):
    f = lambda a: np.ascontiguousarray(np.asarray(a, dtype=np.float32))
    shared = {
        "norm_mix": f(inputs["norm_mix"]), "norm_mlp": f(inputs["norm_mlp"]),
        "norm_final": f(inputs["norm_final"]).reshape(1, 1024),
        "a_re": f(inputs["ssm_a_re"]), "a_im": f(inputs["ssm_a_im"]), "log_dt": f(inputs["ssm_log_dt"]),
        "b_re": f(inputs["ssm_b_re"]), "b_im": f(inputs["ssm_b_im"]),
        "c_re": f(inputs["ssm_c_re"]), "c_im": f(inputs["ssm_c_im"]), "ssm_d": f(inputs["ssm_d"]),
        "w_glu": f(inputs["ssm_w_glu"]), "cw_in": f(inputs["conv_w_in"]),
        "cw": f(inputs["conv_w"]).reshape(6, 1024), "cw_out": f(inputs["conv_w_out"]),
        "w_up": f(inputs["mlp_w_up"]), "w_down": f(inputs["mlp_w_down"]),
    }
    xp = f(inputs["x_prompt"])
    xs = f(inputs["x_sample"]).reshape(128, 1024)
    sre = f(inputs["state_ssm_re"]).reshape(2, 128, 4096)
    sim = f(inputs["state_ssm_im"]).reshape(2, 128, 4096)
    scv = f(inputs["state_conv"])
    maps = []
    for b in range(8):
        m = dict(shared)
        m["xp"] = xp[b]
        m["xs"] = np.ascontiguousarray(xs[16 * b:16 * b + 16])
        m["sre"] = np.ascontiguousarray(sre[:, 16 * b:16 * b + 16])
        m["sim"] = np.ascontiguousarray(sim[:, 16 * b:16 * b + 16])
        m["scv"] = np.ascontiguousarray(scv[:, 16 * b:16 * b + 16])
        maps.append(m)
    return maps


def kernel(**inputs):
    bld = Builder()
    nc = bld.build()
    maps = make_in_maps(inputs)
    res = run_bass_kernel_spmd(nc, maps, core_ids=list(range(8)))
    R = res.results
    y_p = np.stack([R[b]["y_p"] for b in range(8)], 0).astype(np.float32)
    y_s = np.concatenate([R[b]["y_s"] for b in range(8)], 0).reshape(128, 1, 1024).astype(np.float32)
    nre_p = np.stack([R[b]["nre_p"].reshape(2, 64, 64) for b in range(8)], 1).astype(np.float32)
    nim_p = np.stack([R[b]["nim_p"].reshape(2, 64, 64) for b in range(8)], 1).astype(np.float32)
    ncv_p = np.stack([R[b]["ncv_p"] for b in range(8)], 1).astype(np.float32)
    nre_s = np.concatenate([R[b]["nre_s"].reshape(2, 16, 64, 64) for b in range(8)], 1).astype(np.float32)
    nim_s = np.concatenate([R[b]["nim_s"].reshape(2, 16, 64, 64) for b in range(8)], 1).astype(np.float32)
    ncv_s = np.concatenate([R[b]["ncv_s"] for b in range(8)], 1).astype(np.float32)
    return (y_p, y_s, nre_p, nim_p, ncv_p, nre_s, nim_s, ncv_s)
```

```python
import math
from contextlib import ExitStack

import numpy as np
import concourse.bass as bass
import concourse.mybir as mybir
from concourse.bass_utils import run_bass_kernel_spmd

F32 = mybir.dt.float32
BF16 = mybir.dt.bfloat16
AF = mybir.ActivationFunctionType
ALU = mybir.AluOpType

ENGS = ("pe", "act", "dve", "pool", "sp")
NT = 2064
NS = 16
BLK = [(0, 416), (416, 832), (832, 1248), (1248, 1664), (1664, 2064)]
EPS = 1e-6


class Reg:
    __slots__ = ("name", "lw", "rs", "excl")

    def __init__(self, name="", excl=False):
        self.name = name
        self.lw = None
        self.rs = []
        self.excl = excl


class DSem:
    __slots__ = ("sem", "cnt")

    def __init__(self, sem):
        self.sem = sem
        self.cnt = 0


class Op:
    __slots__ = ("eng", "fn", "deps", "pos", "needs_inc", "inc_val", "dsem", "dcnt", "is_dma", "relax")

    def __init__(self, eng, fn):
        self.relax = False
        self.eng = eng
        self.fn = fn
        self.deps = []
        self.pos = 0
        self.needs_inc = False
        self.inc_val = 0
        self.dsem = None
        self.dcnt = 0
        self.is_dma = False


class Prog:
    def __init__(self, nc):
        self.nc = nc
        self.ops = {e: [] for e in ENGS}
        self.final_waits = []
        self.pending_barrier = {e: [] for e in ENGS}

    def _track(self, op, reads, writes):
        deps = op.deps
        ex = [r for r in reads if r.excl]
        if ex:
            reads = [r for r in reads if not r.excl]
            writes = list(writes) + [r for r in ex if r not in writes]
        for r in reads:
            if r.lw is not None:
                deps.append(r.lw)
        for w in writes:
            if w.lw is not None:
                deps.append(w.lw)
            deps.extend(w.rs)
        for r in reads:
            r.rs.append(op)
        for w in writes:
            w.lw = op
            w.rs = []

    def barrier(self):
        lasts = [self.ops[e][-1] for e in ENGS if self.ops[e]]
        for e in ("pe", "act", "dve", "pool", "sp"):
            self.pending_barrier[e] = list(lasts)

    def op(self, eng, fn, reads=(), writes=(), relax=False):
        o = Op(eng, fn)
        o.relax = relax
        if self.pending_barrier[eng]:
            o.deps.extend(self.pending_barrier[eng])
            self.pending_barrier[eng] = []
        self._track(o, reads, writes)
        o.pos = len(self.ops[eng])
        self.ops[eng].append(o)
        return o

    def dma(self, eng, dsem, fn, reads=(), writes=(), n=1):
        o = Op(eng, fn)
        if eng != "pool" and self.pending_barrier[eng]:
            o.deps.extend(self.pending_barrier[eng])
            self.pending_barrier[eng] = []
        o.is_dma = True
        o.dsem = dsem
        dsem.cnt += 16 * n
        o.dcnt = dsem.cnt
        self._track(o, reads, writes)
        o.pos = len(self.ops[eng])
        self.ops[eng].append(o)
        return o

    def _need_wait(self, o, d):
        if d.is_dma or d.eng != o.eng:
            return True
        if o.eng == "pe" and not o.is_dma:
            return False
        if o.is_dma:
            return True
        n = 0
        need = 1 if o.relax else 2
        for x in self.ops[o.eng][d.pos + 1:o.pos]:
            if not x.is_dma:
                n += 1
                if n >= need:
                    return False
        return True

    def emit(self, sems):
        for e in ENGS:
            for o in self.ops[e]:
                o.deps = [d for d in dict.fromkeys(o.deps) if d is not o and self._need_wait(o, d)]
                for d in o.deps:
                    if not d.is_dma:
                        d.needs_inc = True
        for e in ENGS:
            c = 0
            for o in self.ops[e]:
                if o.needs_inc and not o.is_dma:
                    c += 1
                    o.inc_val = c
        nc = self.nc
        with nc.Block() as block:
            def run(ename, eng):
                waited = {}
                for o in self.ops[ename]:
                    for d in o.deps:
                        if d.is_dma:
                            key = ("d", id(d.dsem))
                            sem, val = d.dsem.sem, d.dcnt
                        else:
                            key = ("c", d.eng)
                            sem, val = sems[d.eng], d.inc_val
                        if waited.get(key, 0) >= val:
                            continue
                        waited[key] = val
                        eng.wait_ge(sem, val)
                    r = o.fn(eng)
                    if o.is_dma:
                        for ins in r:
                            ins.then_inc(o.dsem.sem, 16)
                    elif o.needs_inc:
                        r.then_inc(sems[ename], 1)
                if ename == "sp":
                    for ds in self.final_waits:
                        if ds.cnt:
                            eng.wait_ge(ds.sem, ds.cnt)

            @block.tensor
            def _(eng):
                run("pe", eng)

            @block.scalar
            def _(eng):
                run("act", eng)

            @block.vector
            def _(eng):
                run("dve", eng)

            @block.gpsimd
            def _(eng):
                run("pool", eng)

            @block.sync
            def _(eng):
                run("sp", eng)


class T:
    __slots__ = ("ap", "regs")

    def __init__(self, ap, regs):
        self.ap = ap
        self.regs = list(regs) if isinstance(regs, (list, tuple)) else [regs]

    def __getitem__(self, key):
        return T(self.ap[key], self.regs)

    def v(self, fn):
        return T(fn(self.ap), self.regs)


def _rs(v, shape):
    shape = list(shape)
    if len(shape) == 1:
        return v
    if len(shape) == 2:
        return v.rearrange("p (a b) -> p a b", a=shape[0])
    if len(shape) == 3:
        return v.rearrange("p (a b c) -> p a b c", a=shape[0], b=shape[1])
    if len(shape) == 4:
        return v.rearrange("p (a b c d) -> p a b c d", a=shape[0], b=shape[1], c=shape[2])
    raise ValueError(shape)


def _prod(s):
    n = 1
    for x in s:
        n *= x
    return n


AW = 52176
O_XT = 0
O_XN = 16512
O_W = 24768
NSLOT = 5
O_RB = O_W + NSLOT * 2048
O_CONST = O_RB + 16000
RB_SQ = 8256
RB_RS = 12352
RB_LN = 13376


class Builder:
    def __init__(self, layers=(0, 1, 2, 3), s5=True, dbg=False, cut=None):
        self.cut = cut
        self.layers = layers
        self.s5 = s5
        self.dbg = dbg
        self.nc = bass.Bass("TRN2", target_bir_lowering=False)
        self.es = ExitStack()
        self.P = Prog(self.nc)
        self._bank = 0
        self._evac = 0
        self._reserved = set()
        self._norm_done = None
        self._next_norm = None

    def declare(self):
        nc = self.nc

        def din(name, shape):
            return nc.dram_tensor(name, list(shape), F32, kind="ExternalInput").ap()

        def dout(name, shape):
            return nc.dram_tensor(name, list(shape), F32, kind="ExternalOutput").ap()

        self.xp = din("xp", [2048, 1024])
        self.xs = din("xs", [16, 1024])
        self.sre = din("sre", [2, 16, 4096])
        self.sim = din("sim", [2, 16, 4096])
        self.scv = din("scv", [2, 16, 2, 1024])
        self.norm_mix = din("norm_mix", [4, 1024])
        self.norm_mlp = din("norm_mlp", [4, 1024])
        self.norm_final = din("norm_final", [1, 1024])
        self.a_re = din("a_re", [2, 64, 64])
        self.a_im = din("a_im", [2, 64, 64])
        self.log_dt = din("log_dt", [2, 64])
        self.b_re = din("b_re", [2, 64, 64, 16])
        self.b_im = din("b_im", [2, 64, 64, 16])
        self.c_re = din("c_re", [2, 64, 16, 64])
        self.c_im = din("c_im", [2, 64, 16, 64])
        self.ssm_d = din("ssm_d", [2, 1024])
        self.w_glu = din("w_glu", [2, 1024, 2048])
        self.cw_in = din("cw_in", [2, 1024, 3072])
        self.cw = din("cw", [6, 1024])
        self.cw_out = din("cw_out", [2, 1024, 1024])
        self.w_up = din("w_up", [4, 1024, 4096])
        self.w_down = din("w_down", [4, 4096, 1024])

        self.y_p = dout("y_p", [2048, 1024])
        self.y_s = dout("y_s", [16, 1024])
        self.nre_p = dout("nre_p", [2, 4096])
        self.nim_p = dout("nim_p", [2, 4096])
        self.ncv_p = dout("ncv_p", [2, 2, 1024])
        self.nre_s = dout("nre_s", [2, 16, 4096])
        self.nim_s = dout("nim_s", [2, 16, 4096])
        self.ncv_s = dout("ncv_s", [2, 16, 2, 1024])
        if self.dbg:
            self.dbg_xt = dout("dbg_xt", [128, 8 * NT])

        self.scrE = nc.dram_tensor("scrE", [2, 128, 8192], BF16, kind="Internal").ap()
        self.scrT = nc.dram_tensor("scrT", [2, 128, 8192], BF16, kind="Internal").ap()
        self.scrF = nc.dram_tensor("scrF", [2, 128, 8192], BF16, kind="Internal").ap()
        self.scrP = nc.dram_tensor("scrP", [128, 2048], BF16, kind="Internal").ap()

        es = self.es
        self.arena = es.enter_context(nc.sbuf_tensor("arena", [128, AW], F32))
        self.ps = []
        for b in range(8):
            t = es.enter_context(nc.psum_tensor("ps%d" % b, [128, 512], F32))
            self.ps.append(T(t[:, :], Reg("ps%d" % b, excl=True)))
        self.sems = {e: es.enter_context(nc.semaphore("s_" + e)) for e in ENGS}
        self._nds = 0

    def dsem(self, name):
        self._nds += 1
        return DSem(self.es.enter_context(self.nc.semaphore("d_%s_%d" % (name, self._nds))))

    def f32(self, off, shape, regs=None, name=""):
        n = _prod(shape)
        v = self.arena[:, off:off + n]
        return T(_rs(v, shape), regs if regs is not None else Reg(name))

    def bf(self, off, shape, regs=None, name=""):
        n = _prod(shape)
        assert n % 2 == 0
        v = self.arena[:, off:off + n // 2].bitcast(BF16)
        return T(_rs(v, shape), regs if regs is not None else Reg(name))

    def rcols(self, R, k, c0, c1):
        return [R[k][b] for b, (a0, a1) in enumerate(BLK) if a0 < c1 and c0 < a1]

    def bank(self):
        while True:
            b = self._bank
            self._bank = (b + 1) % 8
            if b not in self._reserved:
                return self.ps[b]

    def bank_reserve(self):
        while True:
            b = self._bank
            self._bank = (b + 1) % 8
            if b not in self._reserved:
                self._reserved.add(b)
                return self.ps[b]

    def bank_release(self, banks):
        for t in banks:
            for i, p in enumerate(self.ps):
                if p.regs[0] is t.regs[0]:
                    self._reserved.discard(i)

    def evac_eng(self):
        self._evac ^= 1
        return "act" if self._evac else "dve"

    def tt(self, eng, out, a, b, op, relax=False):
        self.P.op(eng, lambda e: e.tensor_tensor(out=out.ap, in0=a.ap, in1=b.ap, op=op),
                  reads=a.regs + b.regs, writes=out.regs, relax=relax)

    def ts(self, eng, out, a, s1, s2, op0, op1=None):
        rd = list(a.regs)
        s1a = s1.ap if isinstance(s1, T) else s1
        s2a = s2.ap if isinstance(s2, T) else s2
        if isinstance(s1, T):
            rd += s1.regs
        if isinstance(s2, T):
            rd += s2.regs
        if op1 is None:
            self.P.op(eng, lambda e: e.tensor_scalar(out=out.ap, in0=a.ap, scalar1=s1a, scalar2=None, op0=op0),
                      reads=rd, writes=out.regs)
        else:
            self.P.op(eng, lambda e: e.tensor_scalar(out=out.ap, in0=a.ap, scalar1=s1a, scalar2=s2a, op0=op0, op1=op1),
                      reads=rd, writes=out.regs)

    def stt(self, eng, out, a, scalar, b, op0, op1):
        rd = a.regs + b.regs
        sa = scalar.ap if isinstance(scalar, T) else scalar
        if isinstance(scalar, T):
            rd = rd + scalar.regs
        self.P.op(eng, lambda e: e.scalar_tensor_tensor(out=out.ap, in0=a.ap, scalar=sa, in1=b.ap, op0=op0, op1=op1),
                  reads=rd, writes=out.regs)

    def act(self, out, a, func, scale=1.0, bias=0.0):
        rd = list(a.regs)
        sa = scale.ap if isinstance(scale, T) else scale
        ba = bias.ap if isinstance(bias, T) else bias
        if isinstance(scale, T):
            rd += scale.regs
        if isinstance(bias, T):
            rd += bias.regs
        self.P.op("act", lambda e: e.activation(out=out.ap, in_=a.ap, func=func, bias=ba, scale=sa),
                  reads=rd, writes=out.regs)

    def copy(self, eng, out, a):
        if eng == "act":
            self.act(out, a, AF.Copy)
        else:
            self.P.op(eng, lambda e: e.tensor_copy(out=out.ap, in_=a.ap), reads=a.regs, writes=out.regs)

    def memset(self, eng, out, val):
        self.P.op(eng, lambda e: e.memset(out.ap, val), writes=out.regs)

    def mmgroup(self, out, pairs, start=True, stop=True, extra_reads=()):
        rd = []
        for l, r in pairs:
            rd += l.regs + r.regs
        rd += list(extra_reads)
        n = len(pairs)

        def fn(e):
            ins = None
            for i, (l, r) in enumerate(pairs):
                ins = e.matmul(out.ap, lhsT=l.ap, rhs=r.ap, start=(start and i == 0), stop=(stop and i == n - 1),
                               skip_group_check=True)
            return ins
        self.P.op("pe", fn, reads=rd, writes=out.regs)

    def transpose(self, out, a, ident):
        self.P.op("pe", lambda e: e.transpose(out.ap, a.ap, ident.ap), reads=a.regs + ident.regs, writes=out.regs)

    def dma(self, eng, dsem, pairs, reads=(), writes=(), nonc=False):
        n = len(pairs)

        def fn(e):
            r = []
            for o, i in pairs:
                if nonc:
                    r.append(e.dma_start(out=o, in_=i, allow_slow_non_contiguous=True))
                else:
                    r.append(e.dma_start(out=o, in_=i))
            return r
        return self.P.dma(eng, dsem, fn, reads=list(reads), writes=list(writes), n=n)

    def layout(self):
        self.Rxt = [[Reg("xt%d_%d" % (k, b)) for b in range(5)] for k in range(8)]
        self.Rxn = [[Reg("xn%d_%d" % (k, b)) for b in range(5)] for k in range(8)]
        self.XT = [self.f32(O_XT + k * NT, [NT], regs=self.Rxt[k]) for k in range(8)]
        self.XTb = [[T(self.XT[k].ap[:, c0:c1], [self.Rxt[k][b]]) for b, (c0, c1) in enumerate(BLK)] for k in range(8)]
        self.XTall = T(_rs(self.arena[:, O_XT:O_XT + 8 * NT], [8, NT]), [r for k in range(8) for r in self.Rxt[k]])
        self.XN = [self.bf(O_XN + k * (NT // 2), [NT], regs=self.Rxn[k]) for k in range(8)]
        self.XNb = [[T(self.XN[k].ap[:, c0:c1], [self.Rxn[k][b]]) for b, (c0, c1) in enumerate(BLK)] for k in range(8)]
        self.XNall = T(_rs(self.arena[:, O_XN:O_XN + 4 * NT].bitcast(BF16), [8, NT]), [r for k in range(8) for r in self.Rxn[k]])
        self.slot_regs = [Reg("slot%d" % s) for s in range(NSLOT)]
        self.slot_ds = [self.dsem("slot%d" % s) for s in range(NSLOT)]
        o = O_CONST
        self.ident_f = self.f32(o, [128], name="ident_f"); o += 128
        self.ones_f = self.f32(o, [128], name="ones_f"); o += 128
        self.ident_b = self.bf(o, [128], name="ident_b"); o += 64
        self.ones_b = self.bf(o, [128], name="ones_b"); o += 64
        self.chv = self.f32(o, [8, 32], name="chv"); o += 256
        self.coef = []
        for j in range(2):
            d = {}
            for nm in ("l1r", "l1i", "m7r", "m7i"):
                d[nm] = self.f32(o, [32], name="coef"); o += 32
            d["C4"] = self.f32(o, [32, 4], name="coef"); o += 128
            self.coef.append(d)
        assert o <= AW, o
        self.SQ = [self.bf(O_RB + RB_SQ + h * 2048, [8, 512], name="sq%d" % h) for h in range(2)]
        self.RS = [self.f32(O_RB + RB_RS + h * 512, [512], name="rs%d" % h) for h in range(2)]
        self.LN = self.f32(O_RB + RB_LN, [512], name="ln")
        self.HID = [[self.bf(O_RB + h * 4128 + m * (NT // 2), [NT], name="hid%d_%d" % (h, m)) for m in range(4)]
                    for h in range(2)]
        self.R_sbf, self.R_hn, self.R_psh = Reg("sbf"), Reg("hn"), Reg("psh")
        self.R_hn2 = Reg("hn2")
        self.R_y1s = Reg("y1s")
        self.R_xrot = [Reg("xr0"), Reg("xr1")]
        self.R_grot = [Reg("gr0"), Reg("gr1")]
        self.R_hst = [Reg("hst%d" % i) for i in range(4)]
        self.R_t1, self.R_t2, self.R_ts1, self.R_ts2, self.R_zl, self.R_pst = [Reg("s5t") for _ in range(6)]
        self.R_rstdb = Reg("rstdb")
        self.R_ytmp = [Reg("yt0"), Reg("yt1")]
        self.R_em, self.R_h0, self.R_ss, self.R_hld = Reg("em"), Reg("h0"), Reg("ss"), Reg("hld")
        self.R_sg = [Reg("sg0"), Reg("sg1")]
        self.R_tv = [Reg("tv0"), Reg("tv1")]
        self.R_gated = [Reg("gated%d" % k) for k in range(8)]
        self.R_cv, self.R_yc, self.R_gb = Reg("cv"), Reg("yc"), Reg("gb")
        self.R_gc = [Reg("gc0"), Reg("gc1")]
        self.R_lst, self.R_buf, self.R_cvst = Reg("lst"), Reg("buf"), Reg("cvst")
        self.RSTDB_T = self.bf(O_RB + 13888, [NT], regs=[self.R_rstdb])
        self.d_out = [self.dsem("out0"), self.dsem("out1"), self.dsem("out2")]
        self.P.final_waits.extend(self.d_out)
        self.d_misc = self.dsem("misc")
        self.R_scrE = [Reg("scrE0"), Reg("scrE1")]
        self.R_scrT = [Reg("scrT0"), Reg("scrT1")]
        self.R_scrF = [Reg("scrF0"), Reg("scrF1")]
        self.R_scrP = Reg("scrP")
        self.d_hld = self.dsem("hld")
        self.d_pst = self.dsem("pst")
        self.P.final_waits.extend([self.d_hld, self.d_pst])
        self.d_ost = [self.dsem("ost%d" % q) for q in range(8)]
        self.P.final_waits.extend(self.d_ost)

    def slotview(self, s, shape):
        return self.bf(O_W + s * 2048, shape, regs=[self.slot_regs[s]])

    def consts(self):
        self.memset("pool", self.ones_f, 1.0)
        self.P.op("pool", lambda e: e.affine_select(out=self.ident_f.ap, in_=self.ones_f.ap, pattern=[[1, 128]],
                                                   compare_op=ALU.is_equal, fill=0.0, base=0, channel_multiplier=-1),
                  reads=self.ones_f.regs, writes=self.ident_f.regs)
        self.copy("dve", self.ident_b, self.ident_f)
        self.copy("dve", self.ones_b, self.ones_f)
        self.STG1 = self.f32(O_RB + 14208, [1024], name="stg1")
        st = self.STG1
        self.memset("dve", st, 0.0)
        ds = self.dsem("chst")
        self.dma("sp", ds, [(st.ap[0:4, :], self.norm_mix), (st.ap[4:8, :], self.norm_mlp),
                            (st.ap[8:9, :], self.norm_final), (st.ap[9:15, :], self.cw),
                            (st.ap[15:17, :], self.ssm_d)], writes=st.regs)
        pb = self.bank()
        for k in range(8):
            self.transpose(pb[:, k * 32:(k + 1) * 32], st[0:32, k * 128:(k + 1) * 128], self.ident_f[0:32, 0:32])
        self.copy("act", self.chv, pb[:, 0:256].v(lambda a: a.rearrange("p (k r) -> p k r", k=8)))

    def gcol(self, row, k):
        return self.chv[:, k, row:row + 1]

    def load_inputs(self):
        xpv = self.xp.rearrange("(c i) d -> i c d", i=8)
        stg = self.STG1
        self.dma("sp", self.dsem("sst"), [(stg.ap[0:16, :], self.xs)], writes=stg.regs)
        pb = self.bank()
        for k in range(8):
            self.transpose(pb[:, k * 16:(k + 1) * 16], stg[0:16, k * 128:(k + 1) * 128], self.ident_f[0:16, 0:16])
        xs_out = T(self.XTall.ap[:, :, 0:16], [self.Rxt[k][0] for k in range(8)])
        self.copy("act", xs_out, pb[:, 0:128].v(lambda a: a.rearrange("p (k s) -> p k s", k=8)))
        dsi = self.dsem("in")
        for i in range(8):
            for ch in range(2):
                col = 16 + i * 256 + ch * 128
                self.dma("sp", dsi, [(stg.ap, xpv[i, ch * 128:(ch + 1) * 128, :])], writes=stg.regs)
                for half in range(2):
                    pb = self.bank()
                    for kk in range(4):
                        k = half * 4 + kk
                        self.transpose(pb[:, kk * 128:(kk + 1) * 128], stg[:, k * 128:(k + 1) * 128], self.ident_f)
                    dst = T(self.XTall.ap[:, half * 4:half * 4 + 4, col:col + 128],
                            [r for kk in range(4) for r in self.rcols(self.Rxt, half * 4 + kk, col, col + 128)])
                    self.copy("act", dst, pb.v(lambda a: a.rearrange("p (k c) -> p k c", k=4)))

    def norm_s1(self, bi):
        c0, c1 = BLK[bi]
        w = c1 - c0
        h = bi % 2
        xin = T(self.XTall.ap[:, :, c0:c1], [self.Rxt[k][bi] for k in range(8)])
        self.act(self.SQ[h][:, :, 0:w], xin, AF.Square)

    def norm_s2(self, bi, grow, dst, rstd_full):
        sq, rs, ln = self.SQ, self.RS, self.LN
        c0, c1 = BLK[bi]
        w = c1 - c0
        h = bi % 2
        pb = self.bank()
        self.mmgroup(pb[:, 0:w], [(self.ones_b, sq[h][:, k, 0:w]) for k in range(8)])
        self.act(ln[:, 0:w], pb[:, 0:w], AF.Ln, scale=1.0 / 1024.0, bias=EPS)
        self.act(rs[h][:, 0:w], ln[:, 0:w], AF.Exp, scale=-0.5)
        if rstd_full is not None:
            self.copy("dve", rstd_full[:, c0:c1], rs[h][:, 0:w])
        if dst == "inplace":
            for k in range(8):
                self.stt("dve", self.XTb[k][bi], self.XTb[k][bi], self.gcol(grow, k), rs[h][:, 0:w],
                         ALU.mult, ALU.mult)
        elif dst is not None:
            for k in range(8):
                self.stt("dve", self.XNb[k][bi], self.XTb[k][bi], self.gcol(grow, k), rs[h][:, 0:w],
                         ALU.mult, ALU.mult)

    def norm(self, grow, dst=None, rstd_full=None):
        if self._norm_done == (grow, dst is not None, rstd_full is not None):
            self._norm_done = None
            return
        for bi in range(5):
            self.norm_s1(bi)
            self.norm_s2(bi, grow, dst, rstd_full)

    def fused_norm_hook(self, bi, last):
        nn = self._next_norm
        if nn is None:
            return
        grow, dst, rstd = nn
        self.norm_s1(bi)
        if bi >= 1:
            self.norm_s2(bi - 1, grow, dst, rstd)
        if last:
            self.norm_s2(bi, grow, dst, rstd)
            self._norm_done = (grow, dst is not None, rstd is not None)
            self._next_norm = None

    def wq_build(self):
        q = []
        for L in range(4):
            j = L // 2
            if L % 2 == 0:
                if self.s5:
                    for c in (0, 2, 1, 3):
                        q.append((("glu", j, c), self.w_glu[j].rearrange("(k p) n -> p k n", p=128)[:, :, c * 512:(c + 1) * 512],
                                  [8, 512], None, L))
            else:
                for mq in range(2):
                    for part in (1, 2, 0):
                        c0 = part * 1024 + mq * 512
                        q.append((("cin", j, part, mq), self.cw_in[j].rearrange("(k p) n -> p k n", p=128)[:, :, c0:c0 + 512],
                                  [8, 512], None, L))
                for hh in range(2):
                    q.append((("cout", j, hh), self.cw_out[j].rearrange("(k p) n -> p k n", p=128)[:, :, hh * 512:(hh + 1) * 512],
                              [8, 512], None, L))
            for f in range(8):
                q.append((("up", L, f), self.w_up[L].rearrange("(k p) n -> p k n", p=128)[:, :, f * 512:(f + 1) * 512],
                          [8, 512], None, L))
                q.append((("down", L, f), self.w_down[L].rearrange("(f kk p) n -> f p kk n", f=8, p=128)[f],
                          [4, 1024], None, L))
        q = [x for x in q if x[4] in self.layers]
        self.wq = q
        self.wkey = {x[0]: i for i, x in enumerate(q)}
        self.wnext = 0
        self.wslot = {}
        self._ring = 0
        self.slot_cur = [None] * NSLOT
        self.consumed = set()
        self.gates_closed = set()
        for L in (0, 2):
            if self.s5 and L in self.layers:
                self.gates_closed.add(L)

    def _try_emit(self):
        while self.wnext < len(self.wq):
            key, src, shape, allowed, L = self.wq[self.wnext]
            if L in self.gates_closed:
                break
            s = self._ring if allowed is None else allowed[0]
            cur = self.slot_cur[s]
            if cur is not None and cur not in self.consumed:
                break
            if allowed is None:
                self._ring = (self._ring + 1) % NSLOT
            view = self.slotview(s, shape)
            self.wslot[self.wnext] = view
            self.slot_cur[s] = self.wnext
            self.dma("pool", self.slot_ds[s], [(view.ap, src)], writes=view.regs)
            self.wnext += 1

    def wchunk(self, key):
        idx = self.wkey[key]
        self._try_emit()
        assert idx < self.wnext, (key, idx, self.wnext)
        return self.wslot[idx]

    def wdone(self, key):
        self.consumed.add(self.wkey[key])
        self._try_emit()

    def mlp(self, L):
        self.norm(4 + L, dst=self.XN)
        nl = L + 1
        if L == 3:
            self._next_norm = (8, "inplace", None)
        if nl in self.layers and nl < 4:
            if nl % 2 == 1:
                self._next_norm = (nl, self.XN, None)
            elif self.s5 and self.cut is None:
                self._next_norm = (nl, None, self.RSTDB_T)
        hid = self.HID
        for f in range(8):
            wu = self.wchunk(("up", L, f))
            wd = self.wchunk(("down", L, f))
            hh = hid[f % 2]
            for bi, (c0, c1) in enumerate(BLK):
                w = c1 - c0
                for m in range(4):
                    pb = self.bank()
                    self.mmgroup(pb[:, 0:w], [(wu[:, k, m * 128:(m + 1) * 128], self.XNb[k][bi]) for k in range(8)])
                    self.act(hh[m][:, c0:c1], pb[:, 0:w], AF.Relu)
                    self.tt("dve", hh[m][:, c0:c1], hh[m][:, c0:c1], hh[m][:, c0:c1], ALU.mult)
            for bi, (c0, c1) in enumerate(BLK):
                w = c1 - c0
                for mo in range(8):
                    pb = self.bank()
                    self.mmgroup(pb[:, 0:w], [(wd[:, kk, mo * 128:(mo + 1) * 128], hh[kk][:, c0:c1]) for kk in range(4)])
                    self.tt("dve", self.XTb[mo][bi], pb[:, 0:w], self.XTb[mo][bi], ALU.add)
                if f == 7:
                    self.fused_norm_hook(bi, bi == 4)
            self.wdone(("up", L, f))
            self.wdone(("down", L, f))


    def psbf(self, pb):
        return T(pb.ap.bitcast(BF16), pb.regs)

    def s5prep_a(self, j, first):
        A = lambda off, shape, nm: self.f32(O_RB + off, shape, name=nm)
        AL = A(0, [2, 128], "AL")
        o = [256]

        def tab(nm):
            t = A(o[0], [32], nm)
            o[0] += 32
            return t
        ARE, AIM, DTB = tab("are"), tab("aim"), tab("dtb")
        t1, t2, mag, ang, sn, cs, LR, LI = [tab("t") for _ in range(8)]
        lm1, den, rden, nr, ni, WR, WI, n2, rn = [tab("t") for _ in range(9)]
        assert o[0] <= 1024
        PWr = [A(1024 + 64 * i, [32], "pwr") for i in range(16)]
        PWi = [A(1024 + 64 * i + 32, [32], "pwi") for i in range(16)]
        BRE, BIM = A(2048, [32, 16], "bre"), A(2560, [32, 16], "bim")
        BBR, BBI = A(3072, [32, 16], "bbr"), A(3584, [32, 16], "bbi")
        CRE, CIM = A(4096, [32, 16], "cre"), A(4608, [32, 16], "cim")
        CL = [A(5120, [128], "cl0"), A(5248, [128], "cl1")]
        TA = [self.f32(O_XN, [32, 16], name="ta0")]
        TB = [self.f32(O_XN + 512, [32, 16], regs=TA[0].regs)]
        ET = self.bf(O_RB + 5888, [32, 2, 8, 16], name="ET")
        FT = self.bf(O_RB + 9984, [32, 2, 8, 16], name="FT")
        MASK = A(14080, [128], "mask")
        FF = self.bf(O_W, [32, 2, 8, 16], regs=[self.slot_regs[0], self.slot_regs[1]])
        EM = self.bf(O_W + 4096, [64, 2, 64], regs=[self.slot_regs[2], self.slot_regs[3]])
        TM = self.bf(O_W + 4096, [64, 128], regs=[self.slot_regs[2], self.slot_regs[3]])
        PSHp = self.bf(O_W + 8192, [8, 4, 64], regs=[self.slot_regs[4]])
        dso = self.dsem("prepo")
        D, Pl = "dve", "pool"

        self.dma("sp", self.dsem("pl"), [(AL.ap[0:32, 0, :].rearrange("g (h p) -> g h p", h=2), self.a_re[j].rearrange("(h g) p -> g h p", h=2)),
                            (AL.ap[0:32, 1, :].rearrange("g (h p) -> g h p", h=2), self.a_im[j].rearrange("(h g) p -> g h p", h=2))],
                 writes=AL.regs)
        self.dma("sp", self.dsem("pl"), [(DTB.ap[64 * gh:64 * gh + 64, :], self.log_dt[j:j + 1, 32 * gh:32 * gh + 32].broadcast_to([64, 32]))
                            for gh in range(2)], writes=DTB.regs)
        bv_re = self.b_re[j].rearrange("(h g) p x -> h p g x", h=2)
        bv_im = self.b_im[j].rearrange("(h g) p x -> h p g x", h=2)
        self.dma("sp", self.dsem("pl"), [(BRE.ap[64 * gh:64 * gh + 64], bv_re[gh]) for gh in range(2)], writes=BRE.regs)
        self.dma("sp", self.dsem("pl"), [(BIM.ap[64 * gh:64 * gh + 64], bv_im[gh]) for gh in range(2)], writes=BIM.regs)
        pb = self.bank()
        for ri in range(2):
            self.transpose(pb[:, ri * 32:(ri + 1) * 32], AL[0:32, ri, :], self.ident_f[0:32, 0:32])
        self.copy("act", ARE, pb[:, 0:32])
        self.copy("act", AIM, pb[:, 32:64])
        for ci, (csrc, cdst) in enumerate(((self.c_re, CRE), (self.c_im, CIM))):
            cv = csrc[j].rearrange("(h q g) x p -> q (g x) h p", h=2, q=4)
            cflat = cdst.v(lambda a: a.rearrange("p g x -> p (g x)"))
            for q in range(4):
                cl = CL[q % 2]
                self.dma("sp", self.dsem("pl"), [(cl.ap.rearrange("p (h x) -> p h x", h=2), cv[q])], writes=cl.regs)
                pb = self.bank()
                self.transpose(pb[:, 0:128], cl, self.ident_f)
                self.copy("act", cflat[:, q * 128:(q + 1) * 128], pb[:, 0:128])
        self.act(DTB, DTB, AF.Exp)
        self.tt(D, t1, ARE, DTB, ALU.mult)
        self.act(mag, t1, AF.Exp, scale=1.0 / 64.0)
        self.tt(D, ang, AIM, DTB, ALU.mult)
        self.act(sn, ang, AF.Sin, scale=1.0 / 64.0)
        self.ts(D, t2, ang, 1.0 / 64.0, math.pi / 2.0, ALU.mult, ALU.add)
        self.act(cs, t2, AF.Sin)
        self.tt(D, LR, mag, cs, ALU.mult)
        self.tt(D, LI, mag, sn, ALU.mult)
        for _ in range(6):
            self.tt(D, t1, LR, LR, ALU.mult)
            self.tt(D, t2, LI, LI, ALU.mult)
            self.stt(D, LI, LR, 2.0, LI, ALU.mult, ALU.mult)
            self.tt(D, LR, t1, t2, ALU.subtract)

        def cmul(outr, outi, ar, ai, br, bi):
            self.tt(D, t1, ar, br, ALU.mult)
            self.tt(D, t2, ai, bi, ALU.mult)
            self.tt(D, outr, t1, t2, ALU.subtract)
            self.tt(D, t1, ar, bi, ALU.mult)
            self.tt(D, t2, ai, br, ALU.mult)
            self.tt(D, outi, t1, t2, ALU.add)
        K0 = 7
        self.memset(D, PWr[K0], 1.0)
        self.memset(D, PWi[K0], 0.0)
        self.copy(D, PWr[K0 + 1], LR)
        self.copy(D, PWi[K0 + 1], LI)
        for k in range(2, 9):
            cmul(PWr[K0 + k], PWi[K0 + k], PWr[K0 + k - 1], PWi[K0 + k - 1], LR, LI)
        self.tt(D, t1, LR, LR, ALU.mult)
        self.tt(D, t2, LI, LI, ALU.mult)
        self.tt(D, n2, t1, t2, ALU.add)
        self.P.op(D, lambda e: e.reciprocal(rn.ap, n2.ap), reads=n2.regs, writes=rn.regs)
        self.tt(D, PWr[K0 - 1], LR, rn, ALU.mult)
        self.stt(D, PWi[K0 - 1], LI, -1.0, rn, ALU.mult, ALU.mult)
        for k in range(2, 8):
            cmul(PWr[K0 - k], PWi[K0 - k], PWr[K0 - k + 1], PWi[K0 - k + 1], PWr[K0 - 1], PWi[K0 - 1])
        self.ts(D, lm1, LR, -1.0, None, ALU.add)
        self.tt(D, t1, ARE, ARE, ALU.mult)
        self.tt(D, t2, AIM, AIM, ALU.mult)
        self.tt(D, den, t1, t2, ALU.add)
        self.P.op(D, lambda e: e.reciprocal(rden.ap, den.ap), reads=den.regs, writes=rden.regs)
        self.tt(D, t1, lm1, ARE, ALU.mult)
        self.tt(D, t2, LI, AIM, ALU.mult)
        self.tt(D, nr, t1, t2, ALU.add)
        self.tt(D, t1, LI, ARE, ALU.mult)
        self.tt(D, t2, lm1, AIM, ALU.mult)
        self.tt(D, ni, t1, t2, ALU.subtract)
        self.tt(D, WR, nr, rden, ALU.mult)
        self.tt(D, WI, ni, rden, ALU.mult)
        bc = lambda t: t.v(lambda a: a.unsqueeze(2).broadcast_to([128, 32, 16]))
        self.tt(D, TA[0], bc(WR), BRE, ALU.mult)
        self.tt(D, TB[0], bc(WI), BIM, ALU.mult)
        self.tt(D, BBR, TA[0], TB[0], ALU.subtract)
        self.tt(D, TA[0], bc(WR), BIM, ALU.mult)
        self.tt(D, TB[0], bc(WI), BRE, ALU.mult)
        self.tt(D, BBI, TA[0], TB[0], ALU.add)
        cf = self.coef[j]
        self.copy(D, cf["l1r"], PWr[K0 + 1])
        self.copy(D, cf["l1i"], PWi[K0 + 1])
        self.copy(D, cf["m7r"], PWr[K0 - 7])
        self.copy(D, cf["m7i"], PWi[K0 - 7])
        self.copy(D, cf["C4"][:, :, 0], PWr[K0 + 8])
        self.ts(D, cf["C4"][:, :, 1], PWi[K0 + 8], -1.0, None, ALU.mult)
        self.copy(D, cf["C4"][:, :, 2], PWi[K0 + 8])
        self.copy(D, cf["C4"][:, :, 3], PWr[K0 + 8])
        PWall = self.arena[:, O_RB + 1024:O_RB + 2048].rearrange("p (l r g) -> p l r g", l=16, r=2)
        NPW = A(5376, [16, 32], "npw")
        pw_regs = [x.regs[0] for x in PWr] + [x.regs[0] for x in PWi]
        self.P.op(D, lambda e: e.tensor_scalar(out=NPW.ap, in0=PWall[:, :, 1, :], scalar1=-1.0, scalar2=None, op0=ALU.mult),
                  reads=pw_regs, writes=NPW.regs)

        def lv(ap3):
            return ap3.rearrange("p i g -> p g i").unsqueeze(3).broadcast_to([128, 32, 8, 16])
        b4 = lambda t: t.v(lambda a: a.unsqueeze(2).broadcast_to([128, 32, 8, 16]))
        TAd = self.f32(O_XN, [32, 8, 16], regs=TA[0].regs)
        TBd = self.f32(O_XN + 4096, [32, 8, 16], name="TBd")
        Pr_e = T(lv(PWall[:, 14:6:-1, 0, :]), pw_regs)
        Pi_e = T(lv(PWall[:, 14:6:-1, 1, :]), pw_regs)
        self.tt(D, TAd, Pr_e, b4(BBR), ALU.mult)
        self.tt(D, TBd, Pi_e, b4(BBI), ALU.mult)
        self.tt(D, ET[:, :, 0], TAd, TBd, ALU.subtract)
        self.tt(D, TAd, Pr_e, b4(BBI), ALU.mult)
        self.tt(D, TBd, Pi_e, b4(BBR), ALU.mult)
        self.tt(D, ET[:, :, 1], TAd, TBd, ALU.add)
        CQ = self.bf(O_RB + 9984, [32, 2, 16], name="CQ")
        self.copy(D, CQ[:, :, 0, :], CRE)
        self.ts(D, CQ[:, :, 1, :], CIM, -1.0, None, ALU.mult)
        for (dst, l0) in ((FF, 8),):
            Pr_f = T(lv(PWall[:, l0:l0 + 8, 0, :]), pw_regs)
            Pi_f = T(lv(PWall[:, l0:l0 + 8, 1, :]), pw_regs)
            NPi_f = T(lv(NPW.ap[:, l0:l0 + 8, :]), NPW.regs)
            self.tt(D, TAd, b4(CRE), Pr_f, ALU.mult)
            self.tt(D, TBd, b4(CIM), Pi_f, ALU.mult)
            self.tt(D, dst[:, :, 0], TAd, TBd, ALU.subtract)
            self.tt(D, TAd, b4(CRE), NPi_f, ALU.mult)
            self.tt(D, TBd, b4(CIM), Pr_f, ALU.mult)
            self.tt(D, dst[:, :, 1], TAd, TBd, ALU.subtract)
        if first:
            SEL = self.bf(O_RB + 11008, [8, 8, 16], name="SEL")
            for jj in range(8):
                self.P.op(Pl, (lambda jj: (lambda e: e.affine_select(
                    out=SEL.ap[:, jj], in_=self.ones_f.ap.rearrange("p (a b) -> p a b", a=8), pattern=[[-16, 8], [-1, 16]],
                    compare_op=ALU.is_equal, fill=0.0, base=-16 * (7 - jj), channel_multiplier=1)))(jj),
                    reads=self.ones_f.regs, writes=SEL.regs)
            self._sel = SEL
            self.memset(Pl, PSHp, 0.0)
            for a in range(8):
                for b4 in range(4):
                    dst = PSHp[:, a, b4, 16 * b4:16 * b4 + 16]
                    self.P.op(Pl, (lambda dst, a: (lambda e: e.affine_select(
                        out=dst.ap, in_=self.ones_f.ap[:, 0:16], pattern=[[1, 16]], compare_op=ALU.is_equal,
                        fill=0.0, base=16 * a, channel_multiplier=-1)))(dst, a),
                        reads=self.ones_f.regs, writes=dst.regs)
            self.dma("sp", self.dsem("pstP"), [(self.scrP, PSHp.ap.rearrange("p a b m -> p (a b m)"))], reads=PSHp.regs,
                     writes=[self.R_scrP])
        return dict(j=j, ET=ET, CQ=CQ, FF=FF, EM=EM, TM=TM, dso=dso)

    def s5prep_b(self, c):
        j, ET, CQ, FF, EM, TM, dso = c['j'], c['ET'], c['CQ'], c['FF'], c['EM'], c['TM'], c['dso']
        D = 'dve'
        SEL = self._sel
        ETf = ET.v(lambda a: a.rearrange("p g r i x -> p g r (i x)"))
        EMf = EM.v(lambda a: a.rearrange("p g r m -> p (g r m)"))
        for b8 in range(8):
            pb = self.bank()
            pbb = self.psbf(pb)
            for gl in range(8):
                g = b8 * 8 + gl
                gh, gp = g // 32, g % 32
                for ri in range(2):
                    sl = gl * 2 + ri
                    self.transpose(pbb[:, sl * 64:(sl + 1) * 64], ETf[64 * gh:64 * gh + 64, gp, ri, :],
                                   self.ident_b[64 * gh:64 * gh + 64, 64 * gh:64 * gh + 64])
            self.copy(self.evac_eng(), EMf[:, b8 * 1024:(b8 + 1) * 1024], pbb[:, 0:1024])
        self.dma("sp", self.dsem("pstE"), [(self.scrE[j], EMf.ap)], reads=EM.regs, writes=[self.R_scrE[j]])
        KM = self.bf(O_RB + 10496, [64, 16], name="KM")
        for half in range(2):
            pb = self.bank()
            for gp in range(32):
                g = half * 32 + gp
                self.mmgroup(pb[:, gp * 16:(gp + 1) * 16],
                             [(ETf[64 * half:64 * half + 64, gp, ri, :], CQ[64 * half:64 * half + 64, gp, ri, :]) for ri in range(2)])
            self.copy(self.evac_eng(), KM[:, half * 32:(half + 1) * 32, :], pb.v(lambda a: a.rearrange("p (g x) -> p g x", x=16)))
        for b16 in range(16):
            pb = self.bank()
            for gl in range(4):
                g = b16 * 4 + gl
                for jj in range(8):
                    self.mmgroup(pb[:, gl * 128 + jj * 16:gl * 128 + (jj + 1) * 16],
                                 [(SEL[:, jj].v(lambda a: a.rearrange("p a b -> p (a b)")), KM[:, g, :])],
                                 start=(jj == 0), stop=(jj == 7))
            self.copy(self.evac_eng(), TM[:, b16 * 4:(b16 + 1) * 4, :], pb.v(lambda a: a.rearrange("p (g m) -> p g m", g=4)))
        self.dma("sp", dso, [(self.scrT[j], TM.ap.rearrange("p g m -> p (g m)")),
                             (self.scrF[j], FF.ap.rearrange("p g r i x -> p (g r i x)"))],
                 reads=TM.regs + FF.regs, writes=[self.R_scrT[j], self.R_scrF[j]])
        self.P.barrier()

    def _s5_abort(self, L):
        for s in range(4):
            self.slot_cur[s] = None
        self.gates_closed.discard(L)
        for kk in [x[0] for x in self.wq if x[0][0] == "glu" and x[4] == L]:
            self.wchunk(kk)
            self.wdone(kk)
        self.bank_release(list(self.ps))
        self.P.barrier()

    def s5mix(self, L):
        j = L // 2
        cf = self.coef[j]
        RB = O_RB
        SBF = self.bf(RB, [32, 2, 273], regs=[self.R_sbf])
        HNs = [self.bf(RB + 8736, [NT], regs=[self.R_hn]), self.bf(RB + 11720, [NT], regs=[self.R_hn2])]
        PSH = self.bf(RB + 9768, [8, 4, 64], regs=[self.R_psh])
        XROT = [self.bf(RB + 10792 + h * 136, [272], regs=[self.R_xrot[h]]) for h in range(2)]
        GROT = [self.bf(RB + 11064 + h * 136, [272], regs=[self.R_grot[h]]) for h in range(2)]
        HST = [self.f32(RB + 11336 + h * 64, [32, 2], regs=[self.R_hst[h]]) for h in range(4)]
        T1 = self.f32(RB + 11592, [32, 2], regs=[self.R_t1])
        T2 = self.f32(RB + 11656, [32, 2], regs=[self.R_t2])
        TS1 = self.f32(RB + 11720, [32, 16], regs=[self.R_ts1])
        TS2 = self.f32(RB + 12232, [32, 16], regs=[self.R_ts2])
        ZL = self.bf(RB + 13064, [64], regs=[self.R_zl])
        PST = self.f32(RB + 12808, [256], regs=[self.R_pst])
        RSTDB = self.RSTDB_T
        YTMP = [self.f32(RB + 14920 + h * 512, [512], regs=[self.R_ytmp[h]]) for h in range(2)]
        EM = self.bf(O_XN, [64, 2, 64], regs=[self.R_em] + [r for k in range(0, 4) for r in self.Rxn[k]])
        H0 = self.f32(O_XN + 4096, [32, 2, 16], regs=[self.R_h0] + self.Rxn[3] + self.Rxn[4])
        SS = self.f32(O_XN + 5120, [2, 32, 16], regs=[self.R_ss] + self.Rxn[4] + self.Rxn[5])
        HLD = self.f32(O_XN + 6144, [2048], regs=[self.R_hld] + self.Rxn[5] + self.Rxn[6] + self.Rxn[7])
        TM = self.bf(O_W, [64, 128], regs=[self.slot_regs[0], self.slot_regs[1]])
        FF = self.bf(O_W + 4096, [32, 2, 128], regs=[self.slot_regs[2], self.slot_regs[3]])
        for s in range(4):
            self.slot_cur[s] = "s5"
        dsl = [self.dsem("s5l%d" % i) for i in range(6)]
        self.dma("sp", dsl[2], [(EM.ap.rearrange("p g r m -> p (g r m)"), self.scrE[j])], reads=[self.R_scrE[j]], writes=EM.regs)
        self.dma("sp", dsl[0], [(TM.ap.rearrange("p g m -> p (g m)"), self.scrT[j])], reads=[self.R_scrT[j]], writes=TM.regs)
        self.dma("sp", dsl[1], [(FF.ap.rearrange("p g r m -> p (g r m)"), self.scrF[j])], reads=[self.R_scrF[j]], writes=FF.regs)
        self.norm(L, dst=None, rstd_full=RSTDB)
        self.P.barrier()
        self.dma("sp", dsl[3], [(PSH.ap.rearrange("p a b m -> p (a b m)"), self.scrP)], reads=[self.R_scrP], writes=PSH.regs)
        self.memset("dve", ZL, 0.0)
        self.memset("dve", SBF[:, :, :, 16:17], 0.0)
        for ri, src in enumerate((self.sre, self.sim)):
            sv = src[j].rearrange("s (h q r p) -> r s h q p", h=2, q=16, r=2)
            self.dma("sp", dsl[4 + ri], [(HLD.ap[16 * r:16 * r + 16, :].rearrange("s (q h p) -> s h q p", h=2, q=16)[:, h], sv[r][:, h])
                                         for r in range(2) for h in range(2)], writes=HLD.regs)
            pb = self.bank()
            for q in range(16):
                self.transpose(pb[:, q * 32:(q + 1) * 32], HLD[0:32, q * 128:(q + 1) * 128], self.ident_f[0:32, 0:32])
            self.copy("act", H0[:, :, ri, :], pb.v(lambda a: a.rearrange("p (g s) -> p g s", s=16)))

        if self.cut == "A0":
            return self._s5_abort(L)

        def hn_build(k, pool=False):
            HN = HNs[k % 2]
            if pool:
                self.ts("pool", HN, self.XT[k], self.gcol(L, k), None, ALU.mult)
                self.tt("pool", HN, HN, RSTDB, ALU.mult)
            else:
                self.stt("dve", HN, self.XT[k], self.gcol(L, k), RSTDB, ALU.mult, ALU.mult)
            return HN

        def x_build(HN, gl, g, eng=None):
            pX = self.bank()
            self.mmgroup(pX[0:64, 0:16], [(ZL, HN[:, 0:16])])
            self.mmgroup(pX[64:128, 0:16], [(PSH[:, gl, 3, :], HN[:, 0:16])])
            for half in range(2):
                self.mmgroup(pX[64 * half:64 * half + 64, 16:272],
                             [(PSH[:, gl, b4, :], HN[:, 16 + 256 * (4 * half + b4):16 + 256 * (4 * half + b4 + 1)])
                              for b4 in range(4)])
            xr = XROT[g % 2]
            self.copy(eng or self.evac_eng(), xr, pX[:, 0:272])
            return xr

        pSS = [self.bank_reserve(), self.bank_reserve()]

        def s_part(g, xr):
            gh, gp = g // 32, g % 32
            pS = self.bank()
            for ri in range(2):
                self.mmgroup(pS[64 * gh:64 * gh + 64, ri * 256:(ri + 1) * 256], [(EM[:, g, ri, :], xr[:, 16:272])],
                             start=(ri == 0), stop=(ri == 1))
                self.mmgroup(pSS[ri][64 * gh:64 * gh + 64, gp * 16:(gp + 1) * 16], [(EM[:, g, ri, :], xr[:, 0:16])])
            self.copy(self.evac_eng(), SBF[64 * gh:64 * gh + 64, gp, :, 17:273],
                      pS[64 * gh:64 * gh + 64, :].v(lambda a: a.rearrange("p (r c) -> p r c", r=2)))
        pend = None
        for k in range(8):
            HN = hn_build(k)
            for gl in range(8):
                g = 8 * k + gl
                xr = x_build(HN, gl, g)
                if pend is not None:
                    s_part(*pend)
                pend = (g, xr)
        s_part(*pend)
        for ri in range(2):
            self.copy(self.evac_eng(), SS[:, ri, :, :], pSS[ri].v(lambda a: a.rearrange("p (g s) -> p g s", s=16)))
        self.bank_release(pSS)
        self.P.barrier()
        if self.cut == "A":
            return self._s5_abort(L)
        def sample_math():
            bc = lambda t: t.v(lambda a: a.unsqueeze(2).broadcast_to([128, 32, 16]))
            h0r, h0i = H0[:, :, 0, :], H0[:, :, 1, :]
            D = "dve"
            TS1 = T(YTMP[0].ap.rearrange("p (g s) -> p g s", s=16), YTMP[0].regs)
            TS2 = T(YTMP[1].ap.rearrange("p (g s) -> p g s", s=16), YTMP[1].regs)
            self.tt(D, TS1, bc(cf["m7r"]), h0r, ALU.mult)
            self.tt(D, TS2, bc(cf["m7i"]), h0i, ALU.mult)
            self.tt(D, SBF[:, :, 0, 0:16], TS1, TS2, ALU.subtract)
            self.tt(D, TS1, bc(cf["m7r"]), h0i, ALU.mult)
            self.tt(D, TS2, bc(cf["m7i"]), h0r, ALU.mult)
            self.tt(D, SBF[:, :, 1, 0:16], TS1, TS2, ALU.add)
            self.tt(D, TS1, bc(cf["l1r"]), h0r, ALU.mult)
            self.tt(D, SS[:, 0, :, :], SS[:, 0, :, :], TS1, ALU.add)
            self.tt(D, TS2, bc(cf["l1i"]), h0i, ALU.mult)
            self.tt(D, SS[:, 0, :, :], SS[:, 0, :, :], TS2, ALU.subtract)
            self.tt(D, TS1, bc(cf["l1r"]), h0i, ALU.mult)
            self.tt(D, SS[:, 1, :, :], SS[:, 1, :, :], TS1, ALU.add)
            self.tt(D, TS2, bc(cf["l1i"]), h0r, ALU.mult)
            self.tt(D, SS[:, 1, :, :], SS[:, 1, :, :], TS2, ALU.add)

        def sample_out(ri):
            dst = (self.nre_s, self.nim_s)[ri]
            pbs = [self.bank() for _ in range(4)]
            for q in range(16):
                self.transpose(pbs[q // 4][0:32, (q % 4) * 128:(q % 4 + 1) * 128],
                               SS[:, ri, 2 * q:2 * q + 2, :].v(lambda a: a.rearrange('p g s -> p (g s)')), self.ident_f)
            for b in range(4):
                self.copy("act", HLD[0:32, b * 512:(b + 1) * 512], pbs[b][0:32, :])
            dv = dst[j].rearrange("s (h q r p) -> r s h q p", h=2, q=16, r=2)
            self.dma("sp", self.d_hld, [(dv[r][:, h], HLD.ap[16 * r:16 * r + 16, :].rearrange("s (q h p) -> s h q p", h=2, q=16)[:, h])
                                           for r in range(2) for h in range(2)], reads=HLD.regs)
        if self.cut == "S":
            return self._s5_abort(L)
        D = "dve"
        self.P.barrier()
        Rcol = [Reg("sbfc%d" % c) for c in range(256)]
        Rh = [[Reg("hsth") for _ in range(2)] for _ in range(4)]
        Rt = [[Reg("st1"), Reg("st1")], [Reg("st2"), Reg("st2")]]
        hs = lambda i, hf: T(HST[i].ap[:, 16 * hf:16 * hf + 16, :], [Rh[i][hf]])
        t1 = lambda hf: T(T1.ap[:, 16 * hf:16 * hf + 16, :], [Rt[0][hf]])
        t2 = lambda hf: T(T2.ap[:, 16 * hf:16 * hf + 16, :], [Rt[1][hf]])
        P4 = self.f32(RB + 11592, [32, 2, 2], regs=[Rt[1][0]])
        p4 = lambda hf: T(P4.ap[:, 16 * hf:16 * hf + 16], [Rt[1][hf]])
        c4 = lambda hf: cf["C4"][:, 16 * hf:16 * hf + 16, :].v(lambda a: a.rearrange("p g (x y) -> p g x y", x=2))
        T1s = self.f32(RB + 12752, [32, 2], regs=[Rt[0][0]])
        t1 = lambda hf: T(T1s.ap[:, 16 * hf:16 * hf + 16, :], [Rt[0][hf]])
        col = lambda c, hf: T(SBF.ap[:, 16 * hf:16 * hf + 16, :, 17 + c], [Rcol[c]])
        for hf in range(2):
            self.copy("dve", hs(0, hf), col(0, hf))
        Y1S = self.bf(RB + 13128, [64, 16], regs=[self.R_y1s])

        def y1_part(g, xr):
            k, gl = g // 8, g % 8
            pY1 = self.bank()
            self.mmgroup(pY1[:, 0:272], [(TM[:, g, :], xr)])
            self.copy("act", Y1S[:, g, :], pY1[:, 0:16])
            self.copy("act", T(self.XN[k].ap[:, 16 + 256 * gl:16 + 256 * (gl + 1)], self.Rxn[k]), pY1[:, 16:272])
        ypend = None
        HNc = None
        for c in range(1, 256):
            if c == 2:
                sample_math()
            if c == 6:
                sample_out(0)
            if c == 30:
                sample_out(1)
            if c % 4 == 1:
                g = (c - 1) // 4
                if g % 8 == 0:
                    HNc = hn_build(g // 8)
                xr = x_build(HNc, g % 8, g, eng="act")
                if ypend is not None:
                    y1_part(*ypend)
                ypend = (g, xr)
            ip, ic = (c - 1) % 4, c % 4
            for hf in range(2):
                pv = hs(ip, hf).v(lambda a: a.unsqueeze(2).broadcast_to([128, 16, 2, 2]))
                self.tt(D, p4(hf), c4(hf), pv, ALU.mult, relax=True)
            for hf in range(2):
                self.tt(D, t1(hf), p4(hf).v(lambda a: a[:, :, :, 0]), p4(hf).v(lambda a: a[:, :, :, 1]), ALU.add, relax=True)
            for hf in range(2):
                self.tt(D, hs(ic, hf), t1(hf), col(c, hf), ALU.add, relax=True)
            self.copy("act", T(SBF.ap[:, :, :, 17 + c], [Rcol[c]]), T(HST[ic].ap, [Rh[ic][0], Rh[ic][1]]))
        y1_part(*ypend)
        last = T(HST[255 % 4].ap, [Rh[255 % 4][0], Rh[255 % 4][1]])
        pb = self.bank()
        for ri in range(2):
            self.transpose(pb[0:32, ri * 128:(ri + 1) * 128], last[:, :, ri], self.ident_f)
        self.copy("act", PST[0:32, :], pb[0:32, 0:256])
        self.dma("sp", self.d_pst, [(self.nre_p[j].rearrange("(h g p) -> g h p", h=2, g=32), PST.ap[0:32, 0:128].rearrange("g (h p) -> g h p", h=2)),
                                       (self.nim_p[j].rearrange("(h g p) -> g h p", h=2, g=32), PST.ap[0:32, 128:256].rearrange("g (h p) -> g h p", h=2))],
                 reads=PST.regs)
        self.P.barrier()
        if self.cut == "scan":
            return self._s5_abort(L)
        def y_part(g, gr):
            gh, gp = g // 32, g % 32
            k, gl = g // 8, g % 8
            pY = self.bank()
            self.mmgroup(pY[:, 0:16], [(self.ident_b, Y1S[:, g, :])], start=True, stop=False)
            self.mmgroup(pY[:, 16:272], [(self.ident_b, T(self.XN[k].ap[:, 16 + 256 * gl:16 + 256 * (gl + 1)], self.Rxn[k]))],
                         start=False, stop=False)
            self.mmgroup(pY[:, 0:272], [(FF[64 * gh:64 * gh + 64, gp, 0, :], SBF[64 * gh:64 * gh + 64, gp, 0, 0:272]),
                                        (FF[64 * gh:64 * gh + 64, gp, 1, :], SBF[64 * gh:64 * gh + 64, gp, 1, 0:272])],
                         start=False, stop=True)
            self.copy(self.evac_eng(), gr, pY[:, 0:272])
            return gr

        GR4 = GROT + [T(XROT[h].ap, XROT[h].regs) for h in range(2)]

        def u_pair(ga, gra, gb, grb, yP, yS):
            for (g, gr) in ((ga, gra), (gb, grb)):
                gl = g % 8
                half, b4 = gl // 4, gl % 4
                self.mmgroup(yS[64 * half:64 * half + 64, 0:16], [(PSH[:, 7, b4, :], gr[:, 0:16])],
                             start=(b4 == 0), stop=(b4 == 3))
            for jj in range(8):
                for (g, gr) in ((ga, gra), (gb, grb)):
                    gl = g % 8
                    half, b4 = gl // 4, gl % 4
                    self.mmgroup(yP[jj // 2][64 * half:64 * half + 64, (jj % 2) * 256:(jj % 2) * 256 + 256],
                                 [(PSH[:, jj, b4, :], gr[:, 16:272])],
                                 start=(b4 == 0 and jj % 2 == 0), stop=(b4 == 3 and jj % 2 == 1))
        npair = 0
        for k in range(8):
            HN = hn_build(k)
            yP = [self.bank_reserve() for _ in range(4)]
            yS = self.bank_reserve()
            pp = None
            for b4 in range(4):
                ga, gb = 8 * k + b4, 8 * k + 4 + b4
                gra = y_part(ga, GR4[(2 * npair) % 4])
                grb = y_part(gb, GR4[(2 * npair + 1) % 4])
                npair += 1
                if pp is not None:
                    u_pair(*pp, yP, yS)
                pp = (ga, gra, gb, grb)
            u_pair(*pp, yP, yS)
            dk = self.gcol(15 + j, k)
            self.stt("dve", YTMP[0][:, 0:16], HN[:, 0:16], dk, yS[:, 0:16], ALU.mult, ALU.add)
            self.act(T(self.XN[k].ap[:, 0:16], self.rcols(self.Rxn, k, 0, 16)), YTMP[0][:, 0:16], AF.Gelu_apprx_tanh)
            for b in range(4):
                c0 = 16 + 512 * b
                yt = YTMP[(b + 1) % 2]
                self.stt("dve", yt, HN[:, c0:c0 + 512], dk, yP[b], ALU.mult, ALU.add)
                self.act(T(self.XN[k].ap[:, c0:c0 + 512], self.rcols(self.Rxn, k, c0, c0 + 512)), yt, AF.Gelu_apprx_tanh)
            self.bank_release(yP + [yS])
        for s in range(4):
            self.slot_cur[s] = None
        self.gates_closed.discard(L)
        self.P.barrier()
        self._next_norm = (4 + L, self.XN, None)
        SG = [self.f32(RB + 13888 + h * 512, [512], regs=[self.R_sg[h]]) for h in range(2)]
        TV = [self.f32(RB + 13888 + 1024 + h * 512, [512], regs=[self.R_tv[h]]) for h in range(2)]
        n = 0
        for half in range(2):
            wv = self.wchunk(("glu", j, half))
            wg = self.wchunk(("glu", j, 2 + half))
            for bi, (c0, c1) in enumerate(BLK):
                w = c1 - c0
                for ml in range(4):
                    m = half * 4 + ml
                    h = n % 2
                    n += 1
                    pv = self.bank()
                    self.mmgroup(pv[:, 0:w], [(wv[:, kk, ml * 128:(ml + 1) * 128], self.XNb[kk][bi]) for kk in range(8)])
                    pg = self.bank()
                    self.mmgroup(pg[:, 0:w], [(wg[:, kk, ml * 128:(ml + 1) * 128], self.XNb[kk][bi]) for kk in range(8)])
                    self.act(SG[h][:, 0:w], pg[:, 0:w], AF.Sigmoid)
                    self.tt("dve", TV[h][:, 0:w], pv[:, 0:w], SG[h][:, 0:w], ALU.mult)
                    self.tt("dve", self.XTb[m][bi], TV[h][:, 0:w], self.XTb[m][bi], ALU.add)
                if half == 1:
                    self.fused_norm_hook(bi, bi == 4)
            self.wdone(("glu", j, half))
            self.wdone(("glu", j, 2 + half))

    def conv(self, L):
        j = L // 2
        self.norm(L, dst=self.XN)
        self._next_norm = (4 + L, self.XN, None)
        self.P.barrier()
        GATED = [self.bf(O_RB + k * (NT // 2), [NT], regs=self.HID[k // 4][k % 4].regs) for k in range(8)]
        CV = self.f32(O_RB + 8256, [NT], regs=[self.R_cv])
        YC = self.f32(O_RB + 10320, [NT], regs=[self.R_yc])
        GB = self.bf(O_RB + 12384, [NT], regs=[self.R_gb])
        GC = [self.f32(O_RB + 13416 + h * 512, [512], regs=[self.R_gc[h]]) for h in range(2)]
        LST = self.f32(O_RB + 14440, [1024], regs=[self.R_lst])
        BUF = self.f32(O_RB + 15464, [8, 32], regs=[self.R_buf])
        CVST = self.f32(O_RB + 15720, [8, 18], regs=[self.R_cvst])
        self.dma("sp", self.dsem("lst"), [(LST.ap[0:32, :], self.scv[j].rearrange("s r d -> (s r) d"))], writes=LST.regs)
        pb = self.bank()
        for k in range(8):
            self.transpose(pb[:, k * 32:(k + 1) * 32], LST[0:32, k * 128:(k + 1) * 128], self.ident_f[0:32, 0:32])
        self.copy("act", BUF, pb[:, 0:256].v(lambda a: a.rearrange("p (k r) -> p k r", k=8)))
        r0 = 9 + 3 * j
        P1 = 16 + 1792
        P2 = 16 + 1536
        for mq in range(2):
            wgc = self.wchunk(("cin", j, 1, mq))
            wv = self.wchunk(("cin", j, 2, mq))
            wgb = self.wchunk(("cin", j, 0, mq))
            for ml in range(4):
                m = mq * 4 + ml
                for bi, (c0, c1) in enumerate(BLK):
                    w = c1 - c0
                    h = bi % 2
                    pgc = self.bank()
                    self.mmgroup(pgc[:, 0:w], [(wgc[:, k, ml * 128:(ml + 1) * 128], self.XNb[k][bi]) for k in range(8)])
                    pv = self.bank()
                    self.mmgroup(pv[:, 0:w], [(wv[:, k, ml * 128:(ml + 1) * 128], self.XNb[k][bi]) for k in range(8)])
                    pgb = self.bank()
                    self.mmgroup(pgb[:, 0:w], [(wgb[:, k, ml * 128:(ml + 1) * 128], self.XNb[k][bi]) for k in range(8)])
                    self.copy("act", GC[h][:, 0:w], pgc[:, 0:w])
                    self.tt("dve", CV[:, c0:c1], pv[:, 0:w], GC[h][:, 0:w], ALU.mult)
                    self.copy("act", GB[:, c0:c1], pgb[:, 0:w])
                w0 = self.gcol(r0 + 0, m)
                w1 = self.gcol(r0 + 1, m)
                w2 = self.gcol(r0 + 2, m)
                self.ts("dve", YC, CV, w2, None, ALU.mult)
                self.stt("dve", YC[:, 272:NT], CV[:, 16:16 + 1792], w1, YC[:, 272:NT], ALU.mult, ALU.add)
                self.stt("dve", YC[:, 17:272], CV[:, P1:P1 + 255], w1, YC[:, 17:272], ALU.mult, ALU.add)
                self.stt("dve", YC[:, 528:NT], CV[:, 16:16 + 1536], w0, YC[:, 528:NT], ALU.mult, ALU.add)
                self.stt("dve", YC[:, 17:272], CV[:, P2:P2 + 255], w0, YC[:, 17:272], ALU.mult, ALU.add)
                self.stt("dve", YC[:, 273:528], CV[:, P1:P1 + 255], w0, YC[:, 273:528], ALU.mult, ALU.add)
                bk = BUF[:, m, :].v(lambda a: a.rearrange("p (s r) -> p s r", r=2))
                self.stt("dve", YC[:, 0:16], bk[:, :, 1], w1, YC[:, 0:16], ALU.mult, ALU.add)
                self.stt("dve", YC[:, 0:16], bk[:, :, 0], w0, YC[:, 0:16], ALU.mult, ALU.add)
                self.tt("dve", GATED[m], GB, YC, ALU.mult)
                self.copy("act", CVST[:, m, 0:16], CV[:, 0:16])
                self.copy("act", CVST[:, m, 16:17], CV[:, P2 + 255:P2 + 256])
                self.copy("act", CVST[:, m, 17:18], CV[:, P1 + 255:P1 + 256])
            self.wdone(("cin", j, 1, mq))
            self.wdone(("cin", j, 2, mq))
            self.wdone(("cin", j, 0, mq))
        for hh in range(2):
            wo = self.wchunk(("cout", j, hh))
            for bi, (c0, c1) in enumerate(BLK):
                w = c1 - c0
                for ml in range(4):
                    mo = hh * 4 + ml
                    pb = self.bank()
                    self.mmgroup(pb[:, 0:w], [(wo[:, k, ml * 128:(ml + 1) * 128], GATED[k][:, c0:c1]) for k in range(8)])
                    self.tt("dve", self.XTb[mo][bi], pb[:, 0:w], self.XTb[mo][bi], ALU.add)
                if hh == 1:
                    self.fused_norm_hook(bi, bi == 4)
            self.wdone(("cout", j, hh))
        for half in range(2):
            pb = self.bank()
            for kk in range(4):
                k = half * 4 + kk
                self.transpose(pb[0:18, kk * 128:(kk + 1) * 128], CVST[:, k, :], self.ident_f)
            self.copy("act", LST[0:18, half * 512:(half + 1) * 512], pb[0:18, 0:512])
        self.dma("sp", self.d_out[0], [(self.ncv_s[j][:, 1, :], LST.ap[0:16, :]),
                                       (self.ncv_p[j], LST.ap[16:18, :]),
                                       (self.ncv_s[j][:, 0, :], self.scv[j][:, 1, :])], reads=LST.regs)

    def final_out(self):
        self.P.barrier()
        if self._norm_done == (8, True, False):
            self._norm_done = None
        else:
            rsb = self.f32(O_RB + 4608, [NT], name="rstd_fin")
            sq = [self.bf(O_RB + h * 2048, [8, 512], name="sqf%d" % h) for h in range(2)]
            ln = self.f32(O_RB + 4096, [512], name="lnf")
            for bi, (c0, c1) in enumerate(BLK):
                w = c1 - c0
                h = bi % 2
                self.act(sq[h][:, :, 0:w], self.XTall[:, :, c0:c1], AF.Square)
                pb = self.bank()
                self.mmgroup(pb[:, 0:w], [(self.ones_b, sq[h][:, k, 0:w]) for k in range(8)])
                self.act(ln[:, 0:w], pb[:, 0:w], AF.Ln, scale=1.0 / 1024.0, bias=EPS)
                self.act(rsb[:, c0:c1], ln[:, 0:w], AF.Exp, scale=-0.5)
            for k in range(8):
                self.stt("dve", self.XT[k], self.XT[k], self.gcol(8, k), rsb, ALU.mult, ALU.mult)
        if self.dbg:
            self.dma("sp", self.d_out[2], [(self.dbg_xt, self.arena[:, O_XT:O_XT + 8 * NT])], reads=self.XTall.regs)
        ost = [self.f32(O_XN + q * 1024, [1024], name="ost%d" % q) for q in range(8)]
        ypv = self.y_p.rearrange("(c i) d -> i c d", i=8)
        sst = self.f32(O_W, [1024], name="osts")
        pb = self.bank()
        for k in range(4):
            self.transpose(pb[0:16, k * 128:(k + 1) * 128], self.XT[k][:, 0:16], self.ident_f)
        self.copy("act", sst[0:16, 0:512], pb[0:16, 0:512])
        pb2 = self.bank()
        for k in range(4, 8):
            self.transpose(pb2[0:16, (k - 4) * 128:(k - 3) * 128], self.XT[k][:, 0:16], self.ident_f)
        self.copy("dve", sst[0:16, 512:1024], pb2[0:16, 0:512])
        self.dma("sp", self.d_out[2], [(self.y_s, sst.ap[0:16, :])], reads=sst.regs)
        n = 0
        for i in range(8):
            for ch in range(2):
                q = n % 8
                n += 1
                col = 16 + i * 256 + ch * 128
                for half in range(2):
                    pb = self.bank()
                    for kk in range(4):
                        k = half * 4 + kk
                        self.transpose(pb[:, kk * 128:(kk + 1) * 128], self.XT[k][:, col:col + 128], self.ident_f)
                    self.copy(self.evac_eng(), ost[q][:, half * 512:(half + 1) * 512], pb)
                self.dma("sp", self.d_ost[q], [(ypv[i, ch * 128:(ch + 1) * 128, :], ost[q].ap)], reads=ost[q].regs)

    def build(self):
        with self.es:
            self.declare()
            self.layout()
            self.wq_build()
            self.consts()
            ctxs = []
            if self.s5:
                s5l = [L for L in (0, 2) if L in self.layers]
                if s5l:
                    ctxs.append(self.s5prep_a(s5l[0] // 2, True))
            self.load_inputs()
            if self.s5:
                if s5l:
                    self.s5prep_b(ctxs[0])
                for L in s5l[1:]:
                    self.s5prep_b(self.s5prep_a(L // 2, False))
            self.P.barrier()
            for L in range(4):
                if L not in self.layers:
                    continue
                if L % 2 == 1:
                    self.conv(L)
                elif self.s5 and self.cut != "prep":
                    self.s5mix(L)
                elif self.s5:
                    self.gates_closed.discard(L)
                    for kk in [x[0] for x in self.wq if x[0][0] == "glu" and x[4] == L]:
                        self.wchunk(kk)
                        self.wdone(kk)
                self.mlp(L)
            self.final_out()
            self.P.emit(self.sems)
        return self.nc


_IN_KEYS = ["norm_mix", "norm_mlp", "a_re", "a_im", "log_dt", "b_re", "b_im", "c_re", "c_im", "ssm_d",
            "w_glu", "cw_in", "cw_out", "w_up", "w_down"]


def make_in_maps(inputs):
    f = lambda a: np.ascontiguousarray(np.asarray(a, dtype=np.float32))
    shared = {
        "norm_mix": f(inputs["norm_mix"]), "norm_mlp": f(inputs["norm_mlp"]),
        "norm_final": f(inputs["norm_final"]).reshape(1, 1024),
        "a_re": f(inputs["ssm_a_re"]), "a_im": f(inputs["ssm_a_im"]), "log_dt": f(inputs["ssm_log_dt"]),
        "b_re": f(inputs["ssm_b_re"]), "b_im": f(inputs["ssm_b_im"]),
        "c_re": f(inputs["ssm_c_re"]), "c_im": f(inputs["ssm_c_im"]), "ssm_d": f(inputs["ssm_d"]),
        "w_glu": f(inputs["ssm_w_glu"]), "cw_in": f(inputs["conv_w_in"]),
        "cw": f(inputs["conv_w"]).reshape(6, 1024), "cw_out": f(inputs["conv_w_out"]),
        "w_up": f(inputs["mlp_w_up"]), "w_down": f(inputs["mlp_w_down"]),
    }
    xp = f(inputs["x_prompt"])
    xs = f(inputs["x_sample"]).reshape(128, 1024)
    sre = f(inputs["state_ssm_re"]).reshape(2, 128, 4096)
    sim = f(inputs["state_ssm_im"]).reshape(2, 128, 4096)
    scv = f(inputs["state_conv"])
    maps = []
    for b in range(8):
        m = dict(shared)
        m["xp"] = xp[b]
        m["xs"] = np.ascontiguousarray(xs[16 * b:16 * b + 16])
        m["sre"] = np.ascontiguousarray(sre[:, 16 * b:16 * b + 16])
        m["sim"] = np.ascontiguousarray(sim[:, 16 * b:16 * b + 16])
        m["scv"] = np.ascontiguousarray(scv[:, 16 * b:16 * b + 16])
        maps.append(m)
    return maps


def kernel(**inputs):
    bld = Builder()
    nc = bld.build()
    maps = make_in_maps(inputs)
    res = run_bass_kernel_spmd(nc, maps, core_ids=list(range(8)))
    R = res.results
    y_p = np.stack([R[b]["y_p"] for b in range(8)], 0).astype(np.float32)
    y_s = np.concatenate([R[b]["y_s"] for b in range(8)], 0).reshape(128, 1, 1024).astype(np.float32)
    nre_p = np.stack([R[b]["nre_p"].reshape(2, 64, 64) for b in range(8)], 1).astype(np.float32)
    nim_p = np.stack([R[b]["nim_p"].reshape(2, 64, 64) for b in range(8)], 1).astype(np.float32)
    ncv_p = np.stack([R[b]["ncv_p"] for b in range(8)], 1).astype(np.float32)
    nre_s = np.concatenate([R[b]["nre_s"].reshape(2, 16, 64, 64) for b in range(8)], 1).astype(np.float32)
    nim_s = np.concatenate([R[b]["nim_s"].reshape(2, 16, 64, 64) for b in range(8)], 1).astype(np.float32)
    ncv_s = np.concatenate([R[b]["ncv_s"] for b in range(8)], 1).astype(np.float32)
    return (y_p, y_s, nre_p, nim_p, ncv_p, nre_s, nim_s, ncv_s)
```
